# Optimizing a Trainium2 kernel written in Bass

```python
import jax, jax.numpy as jnp
from jax import lax
import numpy as np

D_MODEL = 1024
BATCH = 8
SEQ = 4096
DEPTH = 4

CTX_LEN = 256
GRID_W = 64
N_MIXERS = 3
EPS = 1e-6
D_FF = 4 * D_MODEL
N_MOD = 6

GLA_HEADS = 4
GLA_DK = D_MODEL // 2 // GLA_HEADS
GLA_DV = D_MODEL // GLA_HEADS
GLA_GATE_RANK = 16
GLA_GATE_TAU = 16.0
GLA_CHUNK = 64

RNN_WIDTH = D_MODEL
RNN_BLOCKS = 8
RNN_BLOCK_DIM = RNN_WIDTH // RNN_BLOCKS
CONV_WIDTH = 4
CONV_LEFT = 2
LRU_C = 8.0

HEAD_DIM = 128
Q_HEADS = D_MODEL // HEAD_DIM
KV_HEADS = 2
GROUP = Q_HEADS // KV_HEADS
Q_BLOCK = 128
ROPE_THETA = 10000.0

kernel_name = "hybrid_gla_rglru_gqa_dit_prefix"


def rmsnorm(x, g):
    xf = x.astype(jnp.float32)
    y = xf * lax.rsqrt(jnp.mean(xf * xf, axis=-1, keepdims=True) + EPS)
    return (y * g.astype(jnp.float32)).astype(x.dtype)


def adaln(x, g, shift, scale):
    return rmsnorm(x, g) * (1 + scale) + shift


def flip(a):
    return jnp.flip(a, axis=1)


def gla_chunk_scan(q, k, v, log_g, s0, with_output):
    B_, T, H, _ = q.shape
    n = T // GLA_CHUNK

    def chunks(a):
        return jnp.moveaxis(a.astype(jnp.float32).reshape(B_, n, GLA_CHUNK, H, a.shape[-1]), 1, 0)

    in_chunk_mask = jnp.tril(jnp.ones((GLA_CHUNK, GLA_CHUNK), dtype=bool))[None, :, :, None, None]

    def step(s, inp):
        qc, kc, vc, gc = inp
        b = jnp.cumsum(gc, axis=1)
        b_end = b[:, -1]
        s_new = s * jnp.exp(b_end)[..., None] + jnp.einsum(
            'bshk,bshv->bhkv', kc * jnp.exp(b_end[:, None] - b), vc)
        if not with_output:
            return s_new, None
        rel = jnp.where(in_chunk_mask, b[:, :, None] - b[:, None, :], -jnp.inf)
        scores = jnp.einsum('bthk,btshk,bshk->bhts', qc, jnp.exp(rel), kc)
        o = (jnp.einsum('bhts,bshv->bthv', scores, vc)
             + jnp.einsum('bthk,bhkv->bthv', qc * jnp.exp(b), s))
        return s_new, o

    s_fin, o = lax.scan(step, s0, (chunks(q), chunks(k), chunks(v), chunks(log_g)))
    if with_output:
        o = jnp.moveaxis(o, 0, 1).reshape(B_, T, H, -1).astype(v.dtype)
    return s_fin, o


def gla_mixer(h_c, h_l, w_in, w_up_f, b_f, w_up_b, b_b, norm_g, w_o, need_ctx):
    dq = GLA_HEADS * GLA_DK
    dv = GLA_HEADS * GLA_DV
    cuts = [dq, 2 * dq, 2 * dq + dv, 2 * dq + 2 * dv, 2 * dq + 2 * dv + GLA_GATE_RANK]

    def project(h):
        B_, T, _ = h.shape
        q, k, v, r, gf, gb = jnp.split(h @ w_in, cuts, axis=-1)
        heads = lambda a, d: a.reshape(B_, T, GLA_HEADS, d)
        log_f = jax.nn.log_sigmoid((gf @ w_up_f + b_f).astype(jnp.float32)) / GLA_GATE_TAU
        log_b = jax.nn.log_sigmoid((gb @ w_up_b + b_b).astype(jnp.float32)) / GLA_GATE_TAU
        return (heads(q, GLA_DK) * GLA_DK ** -0.5, heads(k, GLA_DK), heads(v, GLA_DV), r,
                heads(log_f, GLA_DK), heads(log_b, GLA_DK))

    def readout(o, r):
        B_, T = o.shape[:2]
        o = rmsnorm(o, norm_g).reshape(B_, T, GLA_HEADS * GLA_DV)
        return (o * jax.nn.silu(r)) @ w_o

    qc, kc, vc, rc, gfc, gbc = project(h_c)
    ql, kl, vl, rl, gfl, gbl = project(h_l)
    s0 = jnp.zeros((h_l.shape[0], GLA_HEADS, GLA_DK, GLA_DV), jnp.float32)
    s_cf, o_cf = gla_chunk_scan(qc, kc, vc, gfc, s0, need_ctx)
    s_cb, o_cb = gla_chunk_scan(flip(qc), flip(kc), flip(vc), flip(gbc), s0, need_ctx)
    _, o_lf = gla_chunk_scan(ql, kl, vl, gfl, s_cf, True)
    _, o_lb = gla_chunk_scan(flip(ql), flip(kl), flip(vl), flip(gbl), s_cb, True)
    out_l = readout(o_lf + flip(o_lb), rl)
    out_c = readout(o_cf + flip(o_cb), rc) if need_ctx else None
    return out_c, out_l


def depthwise_conv(x, w, b):
    T = x.shape[1]
    xp = jnp.pad(x, ((0, 0), (CONV_LEFT, CONV_WIDTH - 1 - CONV_LEFT), (0, 0)))
    y = b
    for j in range(CONV_WIDTH):
        y = y + xp[:, j:j + T] * w[j]
    return y


def block_diag_linear(x, w, b):
    B_, T, _ = x.shape
    y = jnp.einsum('btnd,nde->btne', x.reshape(B_, T, RNN_BLOCKS, RNN_BLOCK_DIM), w)
    return y.reshape(B_, T, RNN_WIDTH) + b


def _linear_combine(e1, e2):
    a1, b1 = e1
    a2, b2 = e2
    return a1 * a2, a2 * b1 + b2


def rg_lru(x, w_a, b_a, w_x, b_x, lam, h0):
    xf = x.astype(jnp.float32)
    r = jax.nn.sigmoid(block_diag_linear(xf, w_a, b_a))
    i = jax.nn.sigmoid(block_diag_linear(xf, w_x, b_x))
    log_a = -LRU_C * r * jax.nn.softplus(-lam.astype(jnp.float32))
    a = jnp.exp(log_a)
    u = jnp.sqrt(-jnp.expm1(2.0 * log_a)) * (i * xf)
    a_cum, h = lax.associative_scan(_linear_combine, (a, u), axis=1)
    h = h + a_cum * h0[:, None, :]
    return h, h[:, -1]


def lru_mixer(h_c, h_l, w_in, conv_w, conv_b, wa_f, ba_f, wx_f, bx_f, lam_f,
              wa_b, ba_b, wx_b, bx_b, lam_b, w_o, need_ctx):
    def branches(h):
        gate, xb = jnp.split(h @ w_in, 2, axis=-1)
        return gate, depthwise_conv(xb, conv_w, conv_b)

    def readout(hs, gate):
        return (hs * jax.nn.gelu(gate.astype(jnp.float32))).astype(gate.dtype) @ w_o

    gc, xc = branches(h_c)
    gl, xl = branches(h_l)
    h0 = jnp.zeros((h_l.shape[0], RNN_WIDTH), jnp.float32)
    hcf, scf = rg_lru(xc, wa_f, ba_f, wx_f, bx_f, lam_f, h0)
    hcb, scb = rg_lru(flip(xc), wa_b, ba_b, wx_b, bx_b, lam_b, h0)
    hlf, _ = rg_lru(xl, wa_f, ba_f, wx_f, bx_f, lam_f, scf)
    hlb, _ = rg_lru(flip(xl), wa_b, ba_b, wx_b, bx_b, lam_b, scb)
    out_l = readout(hlf + flip(hlb), gl)
    out_c = readout(hcf + flip(hcb), gc) if need_ctx else None
    return out_c, out_l


def rope_2d_tables(T):
    n_rows = T // GRID_W
    row = jnp.repeat(jnp.arange(n_rows), GRID_W)
    col = jnp.tile(jnp.arange(GRID_W), n_rows)
    n_pairs_axis = HEAD_DIM // 4
    inv_freq = ROPE_THETA ** (-jnp.arange(n_pairs_axis, dtype=jnp.float32) / n_pairs_axis)
    ang = jnp.concatenate([row[:, None] * inv_freq, col[:, None] * inv_freq], axis=-1)
    return jnp.cos(ang)[None, :, None, :], jnp.sin(ang)[None, :, None, :]


def apply_rope(x, cos, sin):
    xf = x.astype(jnp.float32)
    x1, x2 = jnp.split(xf, 2, axis=-1)
    return jnp.concatenate([x1 * cos - x2 * sin, x1 * sin + x2 * cos], axis=-1).astype(x.dtype)


def gqa_attend(q, k, v):
    s = jnp.einsum('bqhgd,bkhd->bhgqk', q, k).astype(jnp.float32) * HEAD_DIM ** -0.5
    p = jax.nn.softmax(s, axis=-1).astype(v.dtype)
    return jnp.einsum('bhgqk,bkhd->bqhgd', p, v)


def attn_mixer(h_c, h_l, w_in, q_g, k_g, w_o, need_ctx):
    def project(h):
        B_, T, _ = h.shape
        q, k, v = jnp.split(h @ w_in, [Q_HEADS * HEAD_DIM, (Q_HEADS + KV_HEADS) * HEAD_DIM], axis=-1)
        q = rmsnorm(q.reshape(B_, T, Q_HEADS, HEAD_DIM), q_g)
        k = rmsnorm(k.reshape(B_, T, KV_HEADS, HEAD_DIM), k_g)
        return q, k, v.reshape(B_, T, KV_HEADS, HEAD_DIM)

    qc, kc, vc = project(h_c)
    ql, kl, vl = project(h_l)
    B_, T = h_l.shape[:2]
    cos, sin = rope_2d_tables(T)
    ql = apply_rope(ql, cos, sin)
    kl = apply_rope(kl, cos, sin)
    k_all = jnp.concatenate([kc, kl], axis=1)
    v_all = jnp.concatenate([vc, vl], axis=1)
    qb = jnp.swapaxes(ql.reshape(B_, T // Q_BLOCK, Q_BLOCK, KV_HEADS, GROUP, HEAD_DIM), 0, 1)
    ob = lax.map(lambda qq: gqa_attend(qq, k_all, v_all), qb)
    out_l = jnp.swapaxes(ob, 0, 1).reshape(B_, T, Q_HEADS * HEAD_DIM) @ w_o
    out_c = None
    if need_ctx:
        Tc = h_c.shape[1]
        oc = gqa_attend(qc.reshape(B_, Tc, KV_HEADS, GROUP, HEAD_DIM), kc, vc)
        out_c = oc.reshape(B_, Tc, Q_HEADS * HEAD_DIM) @ w_o
    return out_c, out_l


def sq_relu_mlp(h, w1, w2):
    return jnp.square(jax.nn.relu(h @ w1)) @ w2


def setup_inputs(seed: int = 0) -> dict:
    key = jax.random.key(seed)
    ks = iter(jax.random.split(key, 64))

    def nrm(shape, scale):
        return jax.random.normal(next(ks), shape, jnp.float32) * scale

    def gain(shape):
        return 1.0 + nrm(shape, 0.02)

    def lru_lambda(n):
        a0 = jax.random.uniform(next(ks), (n, RNN_WIDTH), jnp.float32, minval=0.9, maxval=0.999)
        s = a0 ** (1.0 / LRU_C)
        return jnp.log(s) - jnp.log1p(-s)

    n_a, n_b, n_c = (len(range(kind, DEPTH, N_MIXERS)) for kind in range(N_MIXERS))
    D = D_MODEL
    gla_in = 2 * GLA_HEADS * GLA_DK + 2 * GLA_HEADS * GLA_DV + 2 * GLA_GATE_RANK
    gla_kw = GLA_HEADS * GLA_DK
    bd = RNN_BLOCK_DIM ** -0.5
    return {
        "x": nrm((BATCH, SEQ, D), 1.0),
        "c": nrm((BATCH, D), 1.0),
        "ctx": nrm((BATCH, CTX_LEN, D), 1.0),
        "c_ctx": nrm((D,), 1.0),
        "norm_mix_g": gain((DEPTH, D)),
        "norm_mlp_g": gain((DEPTH, D)),
        "w_mod": nrm((DEPTH, D, N_MOD * D), 0.5 * D ** -0.5),
        "b_mod": nrm((DEPTH, N_MOD * D), 0.02),
        "w_mlp1": nrm((DEPTH, D, D_FF), D ** -0.5),
        "w_mlp2": nrm((DEPTH, D_FF, D), D_FF ** -0.5),
        "gla_w_in": nrm((n_a, D, gla_in), D ** -0.5),
        "gla_w_up_f": nrm((n_a, GLA_GATE_RANK, gla_kw), GLA_GATE_RANK ** -0.5),
        "gla_b_f": nrm((n_a, gla_kw), 0.1),
        "gla_w_up_b": nrm((n_a, GLA_GATE_RANK, gla_kw), GLA_GATE_RANK ** -0.5),
        "gla_b_b": nrm((n_a, gla_kw), 0.1),
        "gla_norm_g": gain((n_a, GLA_DV)),
        "gla_w_o": nrm((n_a, GLA_HEADS * GLA_DV, D), (GLA_HEADS * GLA_DV) ** -0.5),
        "lru_w_in": nrm((n_b, D, 2 * RNN_WIDTH), D ** -0.5),
        "lru_conv_w": nrm((n_b, CONV_WIDTH, RNN_WIDTH), 0.5),
        "lru_conv_b": nrm((n_b, RNN_WIDTH), 0.02),
        "lru_wa_f": nrm((n_b, RNN_BLOCKS, RNN_BLOCK_DIM, RNN_BLOCK_DIM), bd),
        "lru_ba_f": nrm((n_b, RNN_WIDTH), 0.1),
        "lru_wx_f": nrm((n_b, RNN_BLOCKS, RNN_BLOCK_DIM, RNN_BLOCK_DIM), bd),
        "lru_bx_f": nrm((n_b, RNN_WIDTH), 0.1),
        "lru_lam_f": lru_lambda(n_b),
        "lru_wa_b": nrm((n_b, RNN_BLOCKS, RNN_BLOCK_DIM, RNN_BLOCK_DIM), bd),
        "lru_ba_b": nrm((n_b, RNN_WIDTH), 0.1),
        "lru_wx_b": nrm((n_b, RNN_BLOCKS, RNN_BLOCK_DIM, RNN_BLOCK_DIM), bd),
        "lru_bx_b": nrm((n_b, RNN_WIDTH), 0.1),
        "lru_lam_b": lru_lambda(n_b),
        "lru_w_o": nrm((n_b, RNN_WIDTH, D), RNN_WIDTH ** -0.5),
        "attn_w_in": nrm((n_c, D, (Q_HEADS + 2 * KV_HEADS) * HEAD_DIM), D ** -0.5),
        "attn_q_g": gain((n_c, HEAD_DIM)),
        "attn_k_g": gain((n_c, HEAD_DIM)),
        "attn_w_o": nrm((n_c, Q_HEADS * HEAD_DIM, D), (Q_HEADS * HEAD_DIM) ** -0.5),
        "final_g": gain((D,)),
    }


def reference(x, c, ctx, c_ctx, norm_mix_g, norm_mlp_g, w_mod, b_mod, w_mlp1, w_mlp2,
              gla_w_in, gla_w_up_f, gla_b_f, gla_w_up_b, gla_b_b, gla_norm_g, gla_w_o,
              lru_w_in, lru_conv_w, lru_conv_b, lru_wa_f, lru_ba_f, lru_wx_f, lru_bx_f, lru_lam_f,
              lru_wa_b, lru_ba_b, lru_wx_b, lru_bx_b, lru_lam_b, lru_w_o,
              attn_w_in, attn_q_g, attn_k_g, attn_w_o, final_g):
    silu_c = jax.nn.silu(c)
    silu_cc = jax.nn.silu(c_ctx)
    for i in range(DEPTH):
        need_ctx = i < DEPTH - 1
        m_l = (silu_c @ w_mod[i] + b_mod[i])[:, None, :]
        m_c = (silu_cc @ w_mod[i] + b_mod[i])[None, None, :]
        sh1, sc1, g1, sh2, sc2, g2 = jnp.split(m_l, N_MOD, axis=-1)
        csh1, csc1, cg1, csh2, csc2, cg2 = jnp.split(m_c, N_MOD, axis=-1)
        h_l = adaln(x, norm_mix_g[i], sh1, sc1)
        h_c = adaln(ctx, norm_mix_g[i], csh1, csc1)
        kind = i % N_MIXERS
        j = i // N_MIXERS
        if kind == 0:
            o_c, o_l = gla_mixer(h_c, h_l, gla_w_in[j], gla_w_up_f[j], gla_b_f[j], gla_w_up_b[j],
                                 gla_b_b[j], gla_norm_g[j], gla_w_o[j], need_ctx)
        elif kind == 1:
            o_c, o_l = lru_mixer(h_c, h_l, lru_w_in[j], lru_conv_w[j], lru_conv_b[j],
                                 lru_wa_f[j], lru_ba_f[j], lru_wx_f[j], lru_bx_f[j], lru_lam_f[j],
                                 lru_wa_b[j], lru_ba_b[j], lru_wx_b[j], lru_bx_b[j], lru_lam_b[j],
                                 lru_w_o[j], need_ctx)
        else:
            o_c, o_l = attn_mixer(h_c, h_l, attn_w_in[j], attn_q_g[j], attn_k_g[j], attn_w_o[j], need_ctx)
        x = x + g1 * o_l
        x = x + g2 * sq_relu_mlp(adaln(x, norm_mlp_g[i], sh2, sc2), w_mlp1[i], w_mlp2[i])
        if need_ctx:
            ctx = ctx + cg1 * o_c
            ctx = ctx + cg2 * sq_relu_mlp(adaln(ctx, norm_mlp_g[i], csh2, csc2), w_mlp1[i], w_mlp2[i])
    return rmsnorm(x, final_g)
```

```python
import contextlib
import numpy as np
import concourse.bass as bass
import concourse.mybir as mybir
from concourse.bass_utils import run_bass_kernel_spmd

F32 = mybir.dt.float32
BF16 = mybir.dt.bfloat16
AF = mybir.ActivationFunctionType
ALU = mybir.AluOpType

D = 1024
NCH = 8
CTX = 256
SEQ = 4096
T = CTX + SEQ
DFF = 4096
NL = 4
EPS = 1e-6

ENGS = ('pe', 'act', 'dve', 'pool', 'sp')


class Prog:
    def __init__(self):
        self.ops = []
        self.eng_ops = {e: [] for e in ENGS}
        self.lane_ops = {}
        self.last_w = {}
        self.readers = {}

    def op(self, eng, fn, reads=(), writes=(), lane=None):
        oid = len(self.ops)
        deps = set()
        for b in reads:
            w = self.last_w.get(b)
            if w is not None:
                deps.add(w)
        for b in writes:
            w = self.last_w.get(b)
            if w is not None:
                deps.add(w)
            for r in self.readers.get(b, {}).values():
                deps.add(r)
        ln = lane if lane is not None else eng
        lst = self.lane_ops.setdefault(ln, [])
        pos = len(lst)
        lst.append(oid)
        self.ops.append(dict(eng=eng, fn=fn, deps=deps, lane=ln, pos=pos,
                             dma=lane is not None, signal=lane is not None))
        self.eng_ops[eng].append(oid)
        for b in writes:
            self.last_w[b] = oid
            self.readers[b] = {}
        for b in reads:
            self.readers.setdefault(b, {})[ln] = oid
        return oid

    def barrier(self):
        lasts = set()
        for lst in self.lane_ops.values():
            for oid in reversed(lst):
                if self.ops[oid]['fn'] is not None:
                    lasts.add(oid)
                    break
        for e in ENGS:
            oid = self.op(e, None)
            self.ops[oid]['deps'] |= lasts

    def resolve(self):
        ops = self.ops
        needed = set()
        for o in ops:
            needed.update(o['deps'])
        clock = {e: {} for e in ENGS}
        snap = {}

        def merge(a, b):
            out = None
            for k, v in b.items():
                if a.get(k, -1) < v:
                    if out is None:
                        out = dict(a)
                    out[k] = v
            return a if out is None else out

        for oid, o in enumerate(ops):
            E = o['eng']
            ck = clock[E]
            waits = []
            for d in sorted(o['deps'], reverse=True):
                do = ops[d]
                if do['eng'] == E and not do['dma'] and E == 'pe':
                    continue
                if ck.get(do['lane'], -1) >= do['pos']:
                    continue
                waits.append(d)
                ck = merge(ck, snap[d])
            if o['dma'] and o['pos'] > 0 and ck.get(o['lane'], -1) < o['pos'] - 1:
                prev = self.lane_ops[o['lane']][o['pos'] - 1]
                waits.append(prev)
                ck = merge(ck, snap[prev])
            clock[E] = ck
            o['waits'] = waits
            for w in waits:
                ops[w]['signal'] = True
            if oid in needed or o['dma']:
                s = dict(ck)
                if s.get(o['lane'], -1) < o['pos']:
                    s[o['lane']] = o['pos']
                snap[oid] = s
        for ln, lst in self.lane_ops.items():
            cnt = 0
            for oid in lst:
                o = ops[oid]
                if o['signal']:
                    cnt += 16 if o['dma'] else 1
                o['semval'] = cnt

    def emit(self, nc):
        self.resolve()
        ops = self.ops
        with contextlib.ExitStack() as es:
            sems = {}
            for ln in self.lane_ops:
                sems[ln] = es.enter_context(nc.semaphore("s_" + ln))
            block = es.enter_context(nc.Block())

            def run(eng_name, eng):
                for oid in self.eng_ops[eng_name]:
                    o = ops[oid]
                    for w in o['waits']:
                        wo = ops[w]
                        eng.wait_ge(sems[wo['lane']], wo['semval'])
                    if o['fn'] is None:
                        continue
                    ins = o['fn']()
                    if o['signal']:
                        ins.then_inc(sems[o['lane']], 16 if o['dma'] else 1)

            @block.tensor
            def _(e):
                run('pe', e)

            @block.scalar
            def _(e):
                run('act', e)

            @block.vector
            def _(e):
                run('dve', e)

            @block.gpsimd
            def _(e):
                run('pool', e)

            @block.sync
            def _(e):
                run('sp', e)


def _fm(v):
    v = np.asarray(v, np.float32)
    lead = int(np.prod(v.shape[:-1])) if v.ndim > 1 else 1
    n = v.shape[-1] // 128
    return np.ascontiguousarray(v.reshape(lead, n, 128).transpose(2, 0, 1).reshape(128, lead * n))


VEC_SPEC = [
    ('nmix', NL * 8), ('nmlp', NL * 8), ('bmod', NL * 48), ('fin', 8),
    ('gla_bf', 2 * 4), ('gla_bb', 2 * 4), ('gla_ng', 2 * 2),
    ('lru_cw', 4 * 8), ('lru_cb', 8), ('ba_f', 8), ('bx_f', 8), ('lam_f', 8),
    ('ba_b', 8), ('bx_b', 8), ('lam_b', 8), ('aq_g', 1), ('ak_g', 1),
]
VEC_OFF = {}
_o = 0
for _n, _w in VEC_SPEC:
    VEC_OFF[_n] = (_o, _w)
    _o += _w
NV = _o


C_IDENT = 0
C_ROT = 128
C_TRIF = 256
C_TRIB = 384
C_MASKF = 512
C_MASKB = 1024
NCONST = 1536
GCK = 128


def make_consts():
    c = np.zeros((128, NCONST), np.float32)
    c[:, C_IDENT:C_IDENT + 128] = np.eye(128, dtype=np.float32)
    rot = np.zeros((128, 128), np.float32)
    for m in range(64):
        rot[m + 64, m] = -1.0
    for m in range(64, 128):
        rot[m - 64, m] = 1.0
    c[:, C_ROT:C_ROT + 128] = rot
    p = np.arange(128)[:, None]
    t = np.arange(128)[None, :]
    c[:, C_TRIF:C_TRIF + 128] = (p <= t).astype(np.float32)
    c[:, C_TRIB:C_TRIB + 128] = (p >= t).astype(np.float32)
    tt = np.arange(512)
    c[:, C_MASKF:C_MASKF + 512] = (tt % GCK != 0).astype(np.float32)[None, :]
    c[:, C_MASKB:C_MASKB + 512] = (tt % GCK != GCK - 1).astype(np.float32)[None, :]
    return c


def make_rope():
    tpos = np.arange(SEQ)
    row = (tpos // 64).astype(np.float32)
    col = (tpos % 64).astype(np.float32)
    inv_freq = (10000.0 ** (-np.arange(32, dtype=np.float32) / 32)).astype(np.float32)
    ang = np.concatenate([row[:, None] * inv_freq, col[:, None] * inv_freq], axis=-1)
    ang = np.concatenate([ang, ang], axis=-1).T
    return np.stack([np.cos(ang), np.sin(ang)]).astype(np.float32)


def pack_vecs(inp):
    parts = {
        'nmix': inp['norm_mix_g'], 'nmlp': inp['norm_mlp_g'], 'bmod': inp['b_mod'], 'fin': inp['final_g'],
        'gla_bf': inp['gla_b_f'], 'gla_bb': inp['gla_b_b'], 'gla_ng': inp['gla_norm_g'],
        'lru_cw': inp['lru_conv_w'][0], 'lru_cb': inp['lru_conv_b'][0],
        'ba_f': inp['lru_ba_f'][0], 'bx_f': inp['lru_bx_f'][0], 'lam_f': inp['lru_lam_f'][0],
        'ba_b': inp['lru_ba_b'][0], 'bx_b': inp['lru_bx_b'][0], 'lam_b': inp['lru_lam_b'][0],
        'aq_g': inp['attn_q_g'][0], 'ak_g': inp['attn_k_g'][0],
    }
    out = np.zeros((128, NV), np.float32)
    for n, w in VEC_SPEC:
        o, _ = VEC_OFF[n]
        a = _fm(parts[n])
        assert a.shape == (128, w), (n, a.shape, w)
        out[:, o:o + w] = a
    return out


class Arena:
    def __init__(self, slab, nwords):
        self.slab = slab
        self.n = nwords
        self.top = 0

    def f32(self, words):
        a = self.slab[:, self.top:self.top + words]
        self.top += words
        assert self.top <= self.n, ("SBUF arena overflow", self.top, self.n)
        return a

    def bf16(self, elems):
        assert elems % 2 == 0
        return self.f32(elems // 2).bitcast(BF16)

    def mark(self):
        return self.top

    def reset(self, m):
        self.top = m


def build(cfg):
    layers = cfg.get('layers', list(range(NL)))
    do_mixer = cfg.get('mixer', True)
    do_final = cfg.get('final', True)
    nc = bass.Bass("TRN2", target_bir_lowering=False)
    P = Prog()

    def dram_in(name, shape):
        return nc.dram_tensor(name, list(shape), F32, kind="ExternalInput").ap()

    xin = dram_in("xT", (D, T))
    c2 = dram_in("c2", (128, 16))
    vecs_d = dram_in("vecs", (128, NV))
    w_mod = dram_in("w_mod", (NL, D, 6 * D))
    w_mlp1 = dram_in("w_mlp1", (NL, D, DFF))
    w_mlp2 = dram_in("w_mlp2", (NL, DFF, D))
    gla_w_in = dram_in("gla_w_in", (2, D, 3104))
    gla_w_up_f = dram_in("gla_w_up_f", (2, 16, 512))
    gla_w_up_b = dram_in("gla_w_up_b", (2, 16, 512))
    gla_w_o = dram_in("gla_w_o", (2, D, D))
    lru_w_in = dram_in("lru_w_in", (1, D, 2 * D))
    lru_wab = {('a', 0): dram_in("lru_wa_f", (1, 8, 128, 128)), ('x', 0): dram_in("lru_wx_f", (1, 8, 128, 128)),
               ('a', 1): dram_in("lru_wa_b", (1, 8, 128, 128)), ('x', 1): dram_in("lru_wx_b", (1, 8, 128, 128))}
    lru_w_o = dram_in("lru_w_o", (1, D, D))
    attn_w_in = dram_in("attn_w_in", (1, D, 1536))
    attn_w_o = dram_in("attn_w_o", (1, D, D))
    consts_d = dram_in("consts", (128, NCONST))
    rope_d = dram_in("rope", (2, 128, SEQ))
    OUTW = SEQ if do_final else T
    out_d = nc.dram_tensor("outT", [D, OUTW], F32, kind="ExternalOutput").ap()
    xs = nc.dram_tensor("xs", [D, T], F32, kind="Internal").ap()
    gx = nc.dram_tensor("gx", [2 * D, T], F32, kind="Internal").ap()
    of_d = nc.dram_tensor("of", [D, T], F32, kind="Internal").ap()

    AW = 53000
    es = contextlib.ExitStack()
    slab = es.enter_context(nc.sbuf_tensor("slab", [128, AW], F32))
    A = Arena(slab, AW)
    psum = [es.enter_context(nc.psum_tensor("ps%d" % i, [128, 512], F32)) for i in range(8)]
    PS = ['ps%d' % i for i in range(8)]

    V = A.f32(NV)
    mods = A.f32(NL * 48 * 2)
    gm = A.f32(NL * 2 * 8 * 2)
    sc = A.f32(16)
    ones_bf = A.bf16(128)
    ones_f = A.f32(128)
    CST = A.f32(NCONST)
    eps_t = A.f32(2)
    eps_ap = eps_t[:, 0:1]
    one_ap = ones_f[:, 0:1]
    ident_bf = A.bf16(128)
    mark0 = A.mark()
    P.op('sp', lambda: nc.sync.dma_start(out=CST, in_=consts_d), writes=['CST'], lane='d_cst')
    P.op('pool', lambda: nc.gpsimd.dma_start(out=ident_bf, in_=consts_d[:, C_IDENT:C_IDENT + 128]),
         writes=['ident'], lane='d_id')

    def vcol(name, i=0, n=1):
        o, w = VEC_OFF[name]
        return V[:, o + i:o + i + n]

    def mod_ap(l, which, c, j):
        k = (l * 48 + which * 8 + c) * 2 + j
        return mods[:, k:k + 1]

    def gm_ap(l, which, c, j):
        k = ((l * 2 + which) * 8 + c) * 2 + j
        return gm[:, k:k + 1]

    P.op('sp', lambda: nc.sync.dma_start(out=V, in_=vecs_d), writes=['V'], lane='d_V')
    P.op('sp', lambda: nc.sync.dma_start(out=sc, in_=c2), writes=['sc'], lane='d_sc')
    P.op('dve', lambda: nc.vector.memset(ones_bf, 1.0), writes=['ones'])
    P.op('dve', lambda: nc.vector.memset(ones_f, 1.0), writes=['ones'])
    P.op('dve', lambda: nc.vector.memset(eps_t, EPS), writes=['ones'])
    P.op('act', lambda: nc.scalar.activation(out=sc, in_=sc, func=AF.Silu), reads=['sc'], writes=['sc'])

    NB = 8
    BW = 6 * D // NB
    wm_bufs = [A.f32(8 * BW) for _ in range(2)]
    sc3 = sc.rearrange("p (k j) -> p k j", j=2)
    blk = 0
    for l in layers:
        for nb in range(NB):
            wb = wm_bufs[blk % 2]
            wbn = 'wm%d' % (blk % 2)
            wb3 = wb.rearrange("p (k n) -> p k n", k=8)
            src = w_mod[l, :, nb * BW:(nb + 1) * BW].rearrange("(k p) n -> p k n", p=128)
            P.op('sp', (lambda wb3=wb3, src=src: nc.sync.dma_start(out=wb3, in_=src)),
                 writes=[wbn], lane='d_' + wbn)
            pb = psum[blk % 2]
            pbn = PS[blk % 2]
            for fc in range(BW // 128):
                for k in range(8):
                    P.op('pe', (lambda pb=pb, wb3=wb3, fc=fc, k=k: nc.tensor.matmul(
                        out=pb[:, fc * 2:fc * 2 + 2], lhsT=wb3[:, k, fc * 128:(fc + 1) * 128],
                        rhs=sc3[:, k, :], start=(k == 0), stop=(k == 7))),
                        reads=[wbn, 'sc'], writes=[pbn])
            nfc = BW // 128
            c0 = nb * nfc
            mo = mods[:, (l * 48 + c0) * 2:(l * 48 + c0 + nfc) * 2].rearrange("p (c j) -> p c j", j=2)
            bo, _ = VEC_OFF['bmod']
            bm = V[:, bo + l * 48 + c0: bo + l * 48 + c0 + nfc].unsqueeze(2).broadcast_to([128, nfc, 2])
            P.op('dve', (lambda mo=mo, pb=pb, bm=bm, nfc=nfc: nc.vector.tensor_tensor(
                out=mo, in0=pb[:, 0:nfc * 2].rearrange("p (c j) -> p c j", j=2), in1=bm, op=ALU.add)),
                reads=[pbn, 'V'], writes=['mods'])
            blk += 1
        for which, gname in ((0, 'nmix'), (1, 'nmlp')):
            go, _ = VEC_OFF[gname]
            g8 = V[:, go + l * 8: go + l * 8 + 8].unsqueeze(2).broadcast_to([128, 8, 2])
            sc_ap = mods[:, (l * 48 + (which * 3 + 1) * 8) * 2:(l * 48 + (which * 3 + 2) * 8) * 2].rearrange(
                "p (c j) -> p c j", j=2)
            gmo = gm[:, ((l * 2 + which) * 8) * 2:((l * 2 + which) * 8 + 8) * 2].rearrange("p (c j) -> p c j", j=2)
            P.op('dve', (lambda gmo=gmo, sc_ap=sc_ap, g8=g8: nc.vector.scalar_tensor_tensor(
                out=gmo, in0=sc_ap, scalar=1.0, in1=g8, op0=ALU.add, op1=ALU.mult)),
                reads=['mods', 'V'], writes=['gm'])
    P.barrier()
    A.reset(mark0)

    def shift_ap(l, which, c, j):
        return mod_ap(l, which * 3 + 0, c, j)

    def gate_ap(l, which, c, j):
        return mod_ap(l, which * 3 + 2, c, j)

    sqb = [A.bf16(512) for _ in range(2)]
    stdb = A.f32(512)
    rstdb = [A.f32(512) for _ in range(2)]
    tmpb = [A.f32(512) for _ in range(2)]
    cnt = {'sq': 0, 'tmp': 0, 'rstd': 0}

    def norm_stage(xt, xtn, W, ps_i, out_h, out_hn, scale_fn, shift_fn, out_dtype_bf16=True):
        ps = psum[ps_i]
        for c in range(8):
            i = cnt['sq'] % 2
            cnt['sq'] += 1
            sb = sqb[i]
            P.op('act', (lambda sb=sb, c=c: nc.scalar.activation(out=sb[:, :W], in_=xt[:, c * W:(c + 1) * W],
                                                                  func=AF.Square)),
                 reads=['%s_%d' % (xtn, c)], writes=['sq%d' % i])
            P.op('pe', (lambda sb=sb, c=c: nc.tensor.matmul(out=ps[:, :W], lhsT=ones_bf, rhs=sb[:, :W],
                                                            start=(c == 0), stop=(c == 7))),
                 reads=['sq%d' % i, 'ones'], writes=[PS[ps_i]])
        P.op('act', lambda: nc.scalar.activation(out=stdb[:, :W], in_=ps[:, :W], func=AF.Ln,
                                                 scale=1.0 / D, bias=eps_ap),
             reads=[PS[ps_i], 'ones'], writes=['std'])
        ri = cnt['rstd'] % 2
        cnt['rstd'] += 1
        rs = rstdb[ri]
        P.op('act', lambda: nc.scalar.activation(out=rs[:, :W], in_=stdb[:, :W], func=AF.Exp, scale=-0.5),
             reads=['std'], writes=['rstd%d' % ri])
        for c in range(8):
            i = cnt['tmp'] % 2
            cnt['tmp'] += 1
            tb = tmpb[i]
            P.op('dve', (lambda tb=tb, c=c: nc.vector.scalar_tensor_tensor(
                out=tb[:, :W], in0=xt[:, c * W:(c + 1) * W], scalar=scale_fn(c), in1=rs[:, :W],
                op0=ALU.mult, op1=ALU.mult)),
                reads=['%s_%d' % (xtn, c), 'rstd%d' % ri, 'gm', 'V'], writes=['tmp%d' % i])
            sh = shift_fn(c)
            if sh is None:
                P.op('act', (lambda tb=tb, c=c: nc.scalar.activation(
                    out=out_h[:, c * W:(c + 1) * W], in_=tb[:, :W], func=AF.Copy)),
                    reads=['tmp%d' % i], writes=['%s_%d' % (out_hn, c)])
            else:
                P.op('act', (lambda tb=tb, c=c, sh=sh: nc.scalar.activation(
                    out=out_h[:, c * W:(c + 1) * W], in_=tb[:, :W], func=AF.Identity, bias=sh, scale=1.0)),
                    reads=['tmp%d' % i, 'mods'], writes=['%s_%d' % (out_hn, c)])

    def blks(name, c0, W):
        return ['%s_%d' % (name, i) for i in range(c0 // 256, (c0 + W - 1) // 256 + 1)]

    cur = {'ap': xin, 'name': 'xin'}

    def mlp_sweep(l, last, src_ap, src_name):
        m0 = A.mark()
        TW = 256
        w1 = A.bf16(8 * DFF)
        w2 = A.bf16(32 * D)
        w1v = w1.rearrange("p (k n) -> p k n", k=8)
        w2v = w2.rearrange("p (k n) -> p k n", k=32)
        for k in range(8 if cfg.get('wload', True) else 0):
            src = w_mlp1[l, k * 128:(k + 1) * 128, :]
            P.op('pool', (lambda k=k, src=src: nc.gpsimd.dma_start(out=w1v[:, k, :], in_=src, max_dma_last_dim=8192)),
                 writes=['w1_%d' % k], lane='d_w1_%d' % (k % 4))
        for k in range(32 if cfg.get('wload', True) else 0):
            src = w_mlp2[l, k * 128:(k + 1) * 128, :]
            P.op('pool', (lambda k=k, src=src: nc.gpsimd.dma_start(out=w2v[:, k, :], in_=src, max_dma_last_dim=8192)),
                 writes=['w2_%d' % k], lane='d_w2_%d' % (k % 4))
        xt = [A.f32(8 * TW) for _ in range(2)]
        hb = [A.bf16(8 * TW) for _ in range(2)]
        h1 = A.bf16(32 * TW)
        rl = [A.f32(TW) for _ in range(2)]
        fo = ([A.f32(8 * TW)] * 2) if last else None
        tiles = []
        if not last:
            tiles.append((0, 256, 1))
        for i in range(cfg.get('ntiles', SEQ // TW)):
            tiles.append((CTX + i * TW, TW, 0))
        n = len(tiles)
        st = {'w1ps': 0, 'w2ps': 0, 'rl': 0}

        def stage_load(ti):
            c0, W, j = tiles[ti]
            b = ti % 2
            x3 = xt[b].rearrange("p (c w) -> p c w", c=8)
            src = src_ap[:, c0:c0 + W].rearrange("(c p) w -> p c w", p=128)
            P.op('sp', (lambda x3=x3, src=src: nc.sync.dma_start(out=x3, in_=src)),
                 reads=blks(src_name, c0, W), writes=['xt%d_%d' % (b, c) for c in range(8)], lane='d_xt%d' % b)

        def stage_norm(ti):
            c0, W, j = tiles[ti]
            b = ti % 2
            norm_stage(xt[b], 'xt%d' % b, W, 0, hb[b], 'h%d' % b,
                       lambda c: gm_ap(l, 1, c, j), lambda c: shift_ap(l, 1, c, j))

        def stage_b(ti):
            c0, W, j = tiles[ti]
            b = ti % 2
            h3 = hb[b].rearrange("p (c w) -> p c w", c=8)
            h13 = h1.rearrange("p (c w) -> p c w", c=32)
            x3 = xt[b].rearrange("p (c w) -> p c w", c=8)
            if ti + 1 < n:
                stage_load(ti + 1)
            for m in range(32):
                pi = 1 + st['w1ps'] % 4
                st['w1ps'] += 1
                ps = psum[pi]
                for k in range(8):
                    P.op('pe', (lambda ps=ps, k=k, m=m: nc.tensor.matmul(
                        out=ps[:, :W], lhsT=w1v[:, k, m * 128:(m + 1) * 128], rhs=h3[:, k, :],
                        start=(k == 0), stop=(k == 7))),
                        reads=['w1_%d' % k, 'h%d_%d' % (b, k)], writes=[PS[pi]])
                ri = st['rl'] % 2
                st['rl'] += 1
                rb = rl[ri]
                P.op('act', (lambda ps=ps, rb=rb: nc.scalar.activation(out=rb[:, :W], in_=ps[:, :W], func=AF.Relu)),
                     reads=[PS[pi]], writes=['rl%d' % ri])
                P.op('dve', (lambda ps=ps, rb=rb, m=m: nc.vector.tensor_tensor(
                    out=h13[:, m, :], in0=ps[:, :W], in1=rb[:, :W], op=ALU.mult)),
                    reads=[PS[pi], 'rl%d' % ri], writes=['h1_%d' % m])
            if ti + 1 < n:
                stage_norm(ti + 1)
            for jc in range(8):
                pi = 5 + st['w2ps'] % 3
                st['w2ps'] += 1
                ps = psum[pi]
                for k in range(32):
                    P.op('pe', (lambda ps=ps, k=k, jc=jc: nc.tensor.matmul(
                        out=ps[:, :W], lhsT=w2v[:, k, jc * 128:(jc + 1) * 128], rhs=h13[:, k, :],
                        start=(k == 0), stop=(k == 31))),
                        reads=['w2_%d' % k, 'h1_%d' % k], writes=[PS[pi]])
                g2 = gate_ap(l, 1, jc, j)
                P.op('dve', (lambda ps=ps, jc=jc, g2=g2: nc.vector.scalar_tensor_tensor(
                    out=x3[:, jc, :], in0=ps[:, :W], scalar=g2, in1=x3[:, jc, :], op0=ALU.mult, op1=ALU.add)),
                    reads=[PS[pi], 'mods', 'xt%d_%d' % (b, jc)], writes=['xt%d_%d' % (b, jc)])
            if not last:
                dst = xs[:, c0:c0 + W].rearrange("(c p) w -> p c w", p=128)
                P.op('sp', (lambda x3=x3, dst=dst: nc.sync.dma_start(out=dst, in_=x3)),
                     reads=['xt%d_%d' % (b, c) for c in range(8)], writes=blks('xs', c0, W), lane='d_xo%d' % b)
            else:
                fb = fo[b]
                fo_off, _ = VEC_OFF['fin']
                norm_stage(xt[b], 'xt%d' % b, W, 0, fb, 'fo',
                           lambda c: V[:, fo_off + c:fo_off + c + 1], lambda c: None)
                f3 = fb.rearrange("p (c w) -> p c w", c=8)
                dst = out_d[:, c0 - CTX:c0 - CTX + W].rearrange("(c p) w -> p c w", p=128)
                P.op('sp', (lambda f3=f3, dst=dst: nc.sync.dma_start(out=dst, in_=f3)),
                     reads=['fo_%d' % c for c in range(8)], writes=['out_%d' % ti], lane='d_fo')
                outbufs.append('out_%d' % ti)

        stage_load(0)
        stage_norm(0)
        for ti in range(n):
            stage_b(ti)
        P.barrier()
        A.reset(m0)

    wl = {'n': 0}

    def wload(dst3, src2d, nk, name, col0, ncols):
        for k in range(nk):
            src = src2d[k * 128:(k + 1) * 128, col0:col0 + ncols]
            P.op('pool', (lambda k=k, src=src: nc.gpsimd.dma_start(out=dst3[:, k, 0:ncols], in_=src,
                                                                  max_dma_last_dim=8192)),
                 writes=['%s_%d' % (name, k)], lane='d_wl%d' % (wl['n'] % 4))
            wl['n'] += 1

    def seq_tiles(W, with_ctx=True):
        tl = [(i * W, W, 1) for i in range(CTX // W)] if with_ctx else []
        if W > CTX and with_ctx:
            tl = [(0, CTX, 1)]
        tl += [(CTX + i * W, W, 0) for i in range(SEQ // W)]
        return tl

    def load_x(xt_ap, xt_name, c0, W, src_ap, src_name, lane):
        x3 = xt_ap[:, :8 * W].rearrange("p (c w) -> p c w", c=8)
        src = src_ap[:, c0:c0 + W].rearrange("(c p) w -> p c w", p=128)
        P.op('sp', (lambda: nc.sync.dma_start(out=x3, in_=src)),
             reads=blks(src_name, c0, W), writes=['%s_%d' % (xt_name, c) for c in range(8)], lane=lane)
        return x3

    def store_x(x3, xt_name, c0, W, lane):
        dst = xs[:, c0:c0 + W].rearrange("(c p) w -> p c w", p=128)
        P.op('sp', (lambda: nc.sync.dma_start(out=dst, in_=x3)),
             reads=['%s_%d' % (xt_name, c) for c in range(8)], writes=blks('xs', c0, W), lane=lane)

    opc = {'n': 0}

    def outproj(y3, yname, wo3, woname, x3, xname, l, j, W, banks):
        for jc in range(8):
            pi = banks[opc['n'] % len(banks)]
            opc['n'] += 1
            ps = psum[pi]
            for k in range(8):
                P.op('pe', (lambda ps=ps, k=k, jc=jc: nc.tensor.matmul(
                    out=ps[:, :W], lhsT=wo3[:, k, jc * 128:(jc + 1) * 128], rhs=y3[:, k, :],
                    start=(k == 0), stop=(k == 7))),
                    reads=['%s_%d' % (woname, k), '%s_%d' % (yname, k)], writes=[PS[pi]])
            g1 = gate_ap(l, 0, jc, j)
            P.op('dve', (lambda ps=ps, jc=jc, g1=g1: nc.vector.scalar_tensor_tensor(
                out=x3[:, jc, :], in0=ps[:, :W], scalar=g1, in1=x3[:, jc, :], op0=ALU.mult, op1=ALU.add)),
                reads=[PS[pi], 'mods', '%s_%d' % (xname, jc)], writes=['%s_%d' % (xname, jc)])

    def lru_layer(l, src_ap, src_name):
        m0 = A.mark()
        W = 512
        win3 = A.bf16(8 * 2048).rearrange("p (k n) -> p k n", k=8)
        wload(win3, lru_w_in[0], 8, 'lwin', 0, 2048)
        xt = [A.f32(8 * W) for _ in range(2)]
        hb = [A.bf16(8 * W) for _ in range(2)]
        stg = [A.f32(4 * W) for _ in range(2)]
        tiles = seq_tiles(W)
        n = len(tiles)
        st = {'ps': 0, 'g': 0}

        def p0_a(ti):
            c0, Wt, j = tiles[ti]
            b = ti % 2
            load_x(xt[b], 'xt%d' % b, c0, Wt, src_ap, src_name, 'd_xt%d' % b)
            norm_stage(xt[b], 'xt%d' % b, Wt, 0, hb[b], 'h%d' % b,
                       lambda c: gm_ap(l, 0, c, j), lambda c: shift_ap(l, 0, c, j))

        def p0_b(ti):
            c0, Wt, j = tiles[ti]
            b = ti % 2
            h3 = hb[b][:, :8 * Wt].rearrange("p (c w) -> p c w", c=8)
            for og in range(4):
                gi = st['g'] % 2
                st['g'] += 1
                sg3 = stg[gi][:, :4 * Wt].rearrange("p (c w) -> p c w", c=4)
                for oi in range(4):
                    oc = og * 4 + oi
                    pi = 1 + st['ps'] % 4
                    st['ps'] += 1
                    ps = psum[pi]
                    for k in range(8):
                        P.op('pe', (lambda ps=ps, k=k, oc=oc: nc.tensor.matmul(
                            out=ps[:, :Wt], lhsT=win3[:, k, oc * 128:(oc + 1) * 128], rhs=h3[:, k, :],
                            start=(k == 0), stop=(k == 7))),
                            reads=['lwin_%d' % k, 'h%d_%d' % (b, k)], writes=[PS[pi]])
                    if oi % 2 == 0:
                        P.op('act', (lambda ps=ps, oi=oi, sg3=sg3: nc.scalar.copy(out=sg3[:, oi, :], in_=ps[:, :Wt])),
                             reads=[PS[pi]], writes=['stg%d_%d' % (gi, oi)])
                    else:
                        P.op('dve', (lambda ps=ps, oi=oi, sg3=sg3: nc.vector.tensor_copy(out=sg3[:, oi, :], in_=ps[:, :Wt])),
                             reads=[PS[pi]], writes=['stg%d_%d' % (gi, oi)])
                dst = gx[og * 512:(og + 1) * 512, c0:c0 + Wt].rearrange("(c p) w -> p c w", p=128)
                P.op('sp', (lambda sg3=sg3, dst=dst: nc.sync.dma_start(out=dst, in_=sg3)),
                     reads=['stg%d_%d' % (gi, oi) for oi in range(4)],
                     writes=['gx%d_%s' % (og, bname) for bname in blks('b', c0, Wt)], lane='d_stg%d' % gi)

        for ti in range(n + 1):
            if ti < n:
                p0_a(ti)
            if ti > 0:
                p0_b(ti - 1)
        P.barrier()
        A.reset(m0)
        lru_dir(l, 0)
        lru_dir(l, 1, src_ap, src_name)

    def lru_dir(l, bwd, src_ap=None, src_name=None):
        m0 = A.mark()
        W = 256
        wa3 = A.bf16(8 * 128).rearrange("p (k n) -> p k n", k=8)
        wx3 = A.bf16(8 * 128).rearrange("p (k n) -> p k n", k=8)
        wload(wa3, lru_wab[('a', bwd)][0].rearrange("n d e -> (n d) e"), 8, 'lwa', 0, 128)
        wload(wx3, lru_wab[('x', bwd)][0].rearrange("n d e -> (n d) e"), 8, 'lwx', 0, 128)
        sfx = '_b' if bwd else '_f'
        nsp8 = A.f32(8)
        carry = A.f32(8)
        lam = V[:, VEC_OFF['lam' + sfx][0]:VEC_OFF['lam' + sfx][0] + 8]
        P.op('act', lambda: nc.scalar.activation(out=nsp8, in_=lam, func=AF.Exp, scale=-1.0), reads=['V'], writes=['nsp8'])
        P.op('act', lambda: nc.scalar.activation(out=nsp8, in_=nsp8, func=AF.Ln, bias=1.0, scale=1.0), reads=['nsp8'], writes=['nsp8'])
        P.op('dve', lambda: nc.vector.tensor_scalar(out=nsp8, in0=nsp8, scalar1=-8.0, scalar2=None, op0=ALU.mult),
             reads=['nsp8'], writes=['nsp8'])
        P.op('dve', lambda: nc.vector.memset(carry, 0.0), writes=['carry'])
        v3 = lambda a: a.rearrange("p (c w) -> p c w", c=8)
        xbh_ = [A.f32(8 * (W + 4)) for _ in range(2)]
        xc_ = [A.f32(8 * W) for _ in range(2)]
        xcb_ = [A.bf16(8 * W) for _ in range(2)]
        rb_ = [A.f32(8 * W) for _ in range(2)]
        ig_ = [A.f32(8 * W) for _ in range(2)]
        sb_ = [A.f32(8 * W) for _ in range(2)]
        hs = A.f32(8 * W)
        hs3 = v3(hs)
        if bwd:
            wo3 = A.bf16(8 * D).rearrange("p (k n) -> p k n", k=8)
            wload(wo3, lru_w_o[0], 8, 'lwo', 0, D)
            hf_ = [A.f32(8 * W)] * 2
            gt_ = [A.f32(8 * W) for _ in range(2)]
            xt_ = [A.f32(8 * W) for _ in range(2)]
            tb_ = [A.f32(8 * W) for _ in range(2)]
            yb = A.bf16(8 * W)
            y3 = v3(yb)
        tiles = seq_tiles(W)
        if bwd:
            tiles = [t for t in tiles if t[2] == 1][::-1] + [t for t in tiles if t[2] == 0][::-1]
        nt = len(tiles)
        st = {'ps': 0}
        bo = lambda nm, c: V[:, VEC_OFF[nm][0] + c:VEC_OFF[nm][0] + c + 1]

        def stage1(ti):
            c0, Wt, j = tiles[ti]
            b = ti % 2
            xbh3 = xbh_[b].rearrange("p (c w) -> p c w", c=8)
            xc, xcb, rb, ig = xc_[b], xcb_[b], rb_[b], ig_[b]
            xc3, xcb3, r3, ig3 = v3(xc), v3(xcb), v3(rb), v3(ig)
            XB = 'xbh%d' % b
            s0, s1 = (0, CTX) if j == 1 else (CTX, T)
            lo, hi = c0 - 2, c0 + Wt + 1
            a0 = 0
            if lo < s0:
                P.op('dve', lambda: nc.vector.memset(xbh3[:, :, 0:2], 0.0), writes=[XB])
                a0 = s0 - lo
                lo = s0
            if hi > s1:
                P.op('dve', lambda: nc.vector.memset(xbh3[:, :, Wt + 2:Wt + 3], 0.0), writes=[XB])
                hi = s1
            src = gx[D:2 * D, lo:hi].rearrange("(c p) w -> p c w", p=128)
            rd = []
            for og in (2, 3):
                rd += ['gx%d_%s' % (og, bname) for bname in blks('b', lo, hi - lo)]
            P.op('sp', (lambda: nc.sync.dma_start(out=xbh3[:, :, a0:a0 + hi - lo], in_=src)),
                 reads=rd, writes=[XB], lane='d_xbh%d' % b)
            if bwd:
                srcg = gx[0:D, c0:c0 + Wt].rearrange("(c p) w -> p c w", p=128)
                rdg = []
                for og in (0, 1):
                    rdg += ['gx%d_%s' % (og, bname) for bname in blks('b', c0, Wt)]
                P.op('sp', (lambda: nc.sync.dma_start(out=v3(gt_[b]), in_=srcg)), reads=rdg, writes=['gt%d' % b],
                     lane='d_gt%d' % b)
                load_x(xt_[b], 'lxt%d' % b, c0, Wt, src_ap, src_name, 'd_lxt%d' % b)
            for c in range(8):
                P.op('dve', (lambda c=c: nc.vector.tensor_scalar(
                    out=xc3[:, c, :], in0=xbh3[:, c, 0:Wt], scalar1=bo('lru_cw', 0 * 8 + c), scalar2=bo('lru_cb', c),
                    op0=ALU.mult, op1=ALU.add)), reads=[XB, 'V'], writes=['xc%d_%d' % (b, c)])
            for jt in range(1, 4):
                for c in range(8):
                    P.op('dve', (lambda c=c, jt=jt: nc.vector.scalar_tensor_tensor(
                        out=xc3[:, c, :], in0=xbh3[:, c, jt:jt + Wt], scalar=bo('lru_cw', jt * 8 + c),
                        in1=xc3[:, c, :], op0=ALU.mult, op1=ALU.add)), reads=[XB, 'V', 'xc%d_%d' % (b, c)],
                        writes=['xc%d_%d' % (b, c)])
            XC = ['xc%d_%d' % (b, c) for c in range(8)]
            P.op('act', lambda: nc.scalar.copy(out=xcb, in_=xc), reads=XC, writes=['xcb%d' % b])
            for (w3, wn, bn, dst3, dn) in ((wa3, 'lwa', 'ba' + sfx, r3, 'r%d' % b), (wx3, 'lwx', 'bx' + sfx, ig3, 'ig%d' % b)):
                for c in range(8):
                    pi = st['ps'] % 4
                    st['ps'] += 1
                    ps = psum[pi]
                    P.op('pe', (lambda ps=ps, c=c, w3=w3: nc.tensor.matmul(
                        out=ps[:, :Wt], lhsT=w3[:, c, :], rhs=xcb3[:, c, :], start=True, stop=True)),
                        reads=['%s_%d' % (wn, c), 'xcb%d' % b], writes=[PS[pi]])
                    P.op('act', (lambda ps=ps, c=c, dst3=dst3, bn=bn: nc.scalar.activation(
                        out=dst3[:, c, :], in_=ps[:, :Wt], func=AF.Sigmoid, bias=bo(bn, c), scale=1.0)),
                        reads=[PS[pi], 'V'], writes=['%s_%d' % (dn, c)])
            for c in range(8):
                P.op('act', (lambda c=c: nc.scalar.activation(out=r3[:, c, :], in_=r3[:, c, :], func=AF.Exp,
                                                              scale=nsp8[:, c:c + 1])),
                     reads=['r%d_%d' % (b, c), 'nsp8'], writes=['r%d_%d' % (b, c)])
            if bwd:
                gt, tb = gt_[b], tb_[b]
                GT, TB = 'gt%d' % b, 'tb%d' % b
                P.op('act', lambda: nc.scalar.activation(out=tb, in_=gt, func=AF.Square), reads=[GT], writes=[TB])
                P.op('act', lambda: nc.scalar.activation(out=tb, in_=tb, func=AF.Identity, scale=0.044715, bias=one_ap),
                     reads=[TB, 'ones'], writes=[TB])
                P.op('dve', lambda: nc.vector.tensor_tensor(out=tb, in0=tb, in1=gt, op=ALU.mult), reads=[TB, GT], writes=[TB])
                P.op('act', lambda: nc.scalar.activation(out=tb, in_=tb, func=AF.Sigmoid, scale=1.5957691216057308),
                     reads=[TB], writes=[TB])

        def stage1b(ti):
            b = ti % 2
            rb, sb = rb_[b], sb_[b]
            RA = ['r%d_%d' % (b, c) for c in range(8)]
            P.op('act', lambda: nc.scalar.activation(out=sb, in_=rb, func=AF.Square), reads=RA, writes=['s%d' % b])
            P.op('act', lambda: nc.scalar.activation(out=sb, in_=sb, func=AF.Sqrt, scale=-1.0, bias=1.0),
                 reads=['s%d' % b], writes=['s%d' % b])

        def stage2(ti):
            c0, Wt, j = tiles[ti]
            b = ti % 2
            xc, rb, ig, ub = xc_[b], rb_[b], ig_[b], sb_[b]
            r3, u3 = v3(rb), v3(ub)
            XC = ['xc%d_%d' % (b, c) for c in range(8)]
            IG = ['ig%d_%d' % (b, c) for c in range(8)]
            if bwd:
                srcf = of_d[:, c0:c0 + Wt].rearrange("(c p) w -> p c w", p=128)
                P.op('sp', (lambda: nc.sync.dma_start(out=v3(hf_[0]), in_=srcf)),
                     reads=blks('of', c0, Wt), writes=['hf'], lane='d_hf')
            P.op('dve', lambda: nc.vector.tensor_tensor(out=ig, in0=ig, in1=xc, op=ALU.mult), reads=IG + XC, writes=IG)
            P.op('dve', lambda: nc.vector.tensor_tensor(out=ub, in0=ub, in1=ig, op=ALU.mult), reads=['s%d' % b] + IG,
                 writes=['s%d' % b])
            for c in range(8):
                if bwd:
                    P.op('dve', (lambda c=c: nc.vector.tensor_tensor_scan(
                        out=hs3[:, c, ::-1], data0=r3[:, c, ::-1], data1=u3[:, c, ::-1], initial=carry[:, c:c + 1],
                        op0=ALU.mult, op1=ALU.add)), reads=['r%d_%d' % (b, c), 's%d' % b, 'carry'], writes=['hs_%d' % c])
                else:
                    P.op('dve', (lambda c=c: nc.vector.tensor_tensor_scan(
                        out=hs3[:, c, :], data0=r3[:, c, :], data1=u3[:, c, :], initial=carry[:, c:c + 1],
                        op0=ALU.mult, op1=ALU.add)), reads=['r%d_%d' % (b, c), 's%d' % b, 'carry'], writes=['hs_%d' % c])
            HS = ['hs_%d' % c for c in range(8)]
            last_col = 0 if bwd else Wt - 1
            P.op('dve', lambda: nc.vector.tensor_copy(out=carry, in_=hs3[:, :, last_col]), reads=HS, writes=['carry'])
            if not bwd:
                dst = of_d[:, c0:c0 + Wt].rearrange("(c p) w -> p c w", p=128)
                P.op('sp', (lambda: nc.sync.dma_start(out=dst, in_=hs3)), reads=HS, writes=blks('of', c0, Wt), lane='d_of')
                return
            hf, gt = hf_[b], gt_[b]
            x3 = v3(xt_[b])
            HF, GT = 'hf', 'gt%d' % b
            tb = tb_[b]
            TB = 'tb%d' % b
            P.op('dve', lambda: nc.vector.tensor_tensor(out=hf, in0=hf, in1=hs, op=ALU.add), reads=HS + [HF], writes=[HF])
            P.op('dve', lambda: nc.vector.tensor_tensor(out=hf, in0=hf, in1=gt, op=ALU.mult), reads=[HF, GT], writes=[HF])
            P.op('dve', lambda: nc.vector.tensor_tensor(out=yb, in0=hf, in1=tb, op=ALU.mult), reads=[HF, TB],
                 writes=['ly_%d' % c for c in range(8)])
            outproj(y3, 'ly', wo3, 'lwo', x3, 'lxt%d' % b, l, j, Wt, [4, 5, 6, 7])
            store_x(x3, 'lxt%d' % b, c0, Wt, 'd_lxo')

        stage1(0)
        stage1b(0)
        for ti in range(nt):
            if ti + 1 < nt:
                stage1(ti + 1)
            stage2(ti)
            if ti + 1 < nt:
                stage1b(ti + 1)
        P.barrier()
        A.reset(m0)

    def gla_layer(l, jl, src_ap, src_name, need_ctx):
        m0 = A.mark()
        win3 = A.bf16(8 * 3104).rearrange("p (k n) -> p k n", k=8)
        wload(win3, gla_w_in[jl], 8, 'gwin', 0, 3104)
        wo3 = A.bf16(8 * D).rearrange("p (k n) -> p k n", k=8)
        m1 = A.mark()
        gla_dir(l, jl, 0, src_ap, src_name, need_ctx, win3, wo3)
        A.reset(m1)
        gla_dir(l, jl, 1, src_ap, src_name, need_ctx, win3, wo3)
        P.barrier()
        A.reset(m0)

    def gla_dir(l, jl, bwd, src_ap, src_name, need_ctx, win3, wo3):
        W = 256
        NCK = W // GCK
        wup = A.f32(512)
        wup_src = (gla_w_up_b if bwd else gla_w_up_f)[jl]
        P.op('sp', lambda: nc.sync.dma_start(out=wup[0:16, :], in_=wup_src), writes=['gwup'], lane='d_gwup')
        negb = A.f32(4)
        bname = 'gla_bb' if bwd else 'gla_bf'
        bsrc = V[:, VEC_OFF[bname][0] + jl * 4:VEC_OFF[bname][0] + jl * 4 + 4]
        P.op('dve', lambda: nc.vector.tensor_scalar(out=negb, in0=bsrc, scalar1=-1.0, scalar2=None, op0=ALU.mult),
             reads=['V'], writes=['negb'])
        if not bwd:
            wload(wo3, gla_w_o[jl], 8, 'gwo', 0, D)
        xts = [A.f32(8 * W) for _ in range(2)]
        hbs = [A.bf16(8 * W) for _ in range(2)]
        q32, k32, spb, Bb, Eb = (A.f32(4 * W) for _ in range(5))
        gfs = A.f32(W)
        qt, kt, kpT = (A.bf16(4 * W) for _ in range(3))
        kptok = A.bf16(2 * 512)
        vtok = A.bf16(2 * 1024)
        Ab = [A.bf16(512) for _ in range(2)]
        S32 = A.f32(4 * 256)
        Sbf = A.bf16(4 * 256)
        E3 = A.f32(4 * NCK)
        ot = A.f32(8 * W)
        v8 = lambda a: a.rearrange("p (c w) -> p c w", c=8)
        v4 = lambda a: a.rearrange("p (c w) -> p c w", c=4)
        ot3 = v8(ot)
        q3, k3, sp3, B3, E3v = v4(q32), v4(k32), v4(spb), v4(Bb), v4(Eb)
        qt3, kt3, kp3 = v4(qt), v4(kt), v4(kpT)
        B4 = Bb.rearrange("p (h c t) -> p h c t", h=4, t=GCK)
        E4 = Eb.rearrange("p (h c t) -> p h c t", h=4, t=GCK)
        kptok3 = kptok.rearrange("p (b n) -> p b n", b=2)
        vtok3 = vtok.rearrange("p (b n) -> p b n", b=2)
        S3 = v4(S32)
        Sb3 = v4(Sbf)
        E33 = E3.rearrange("p (h c) -> p h c", h=4)
        ofls = [A.f32(8 * W) for _ in range(2)]
        rsb = A.f32(8 * W)
        yb = A.bf16(8 * W)
        sq8 = A.bf16(8 * W)
        rs3, y3, sq83 = v8(rsb), v8(yb), v8(sq8)
        P.op('dve', lambda: nc.vector.memset(S32, 0.0), writes=['S32_%d' % h for h in range(4)])
        P.op('dve', lambda: nc.vector.memset(Sbf, 0.0), writes=['Sbf_%d' % h for h in range(4)])
        tiles = seq_tiles(W)
        if bwd:
            tiles = [t for t in tiles if t[2] == 1][::-1] + [t for t in tiles if t[2] == 0][::-1]
        st = {'pp': 0, 'ab': 0}
        maskc = C_MASKB if bwd else C_MASKF
        tric = C_TRIB if bwd else C_TRIF
        tri = CST[:, tric:tric + 128].unsqueeze(1).broadcast_to([128, 4, 128])
        gcol = 3088 if bwd else 3072
        ngo = VEC_OFF['gla_ng'][0] + jl * 2

        def pbank():
            pi = 1 + st['pp'] % 2
            st['pp'] += 1
            return pi

        nt = len(tiles)

        def stage_a(ti):
            c0, Wt, j = tiles[ti]
            bb = ti % 2
            need_out = (j == 0) or need_ctx
            load_x(xts[bb], 'gxt%d' % bb, c0, Wt, src_ap, src_name, 'd_gxt%d' % bb)
            if bwd and need_out:
                srcf = of_d[:, c0:c0 + Wt].rearrange("(c p) w -> p c w", p=128)
                ofl3 = v8(ofls[bb])
                P.op('sp', (lambda srcf=srcf, ofl3=ofl3: nc.sync.dma_start(out=ofl3, in_=srcf)),
                     reads=blks('of', c0, Wt), writes=['ofl%d' % bb], lane='d_ofl%d' % bb)
            norm_stage(xts[bb], 'gxt%d' % bb, Wt, 0, hbs[bb], 'gh%d' % bb,
                       (lambda c, j=j: gm_ap(l, 0, c, j)), (lambda c, j=j: shift_ap(l, 0, c, j)))

        pre = set()

        def proj_qk(ti, on_dve):
            c0, Wt, j = tiles[ti]
            bb = ti % 2
            h3 = v8(hbs[bb])
            GH = 'gh%d' % bb
            for (col, dst3, dn, scl) in ((0, q3, 'q32', 128.0 ** -0.5), (512, k3, 'k32', 1.0)):
                for h in range(4):
                    pi = pbank()
                    ps = psum[pi]
                    for k in range(8):
                        P.op('pe', (lambda ps=ps, k=k, h=h, col=col: nc.tensor.matmul(
                            out=ps[:, :Wt], lhsT=win3[:, k, col + h * 128:col + (h + 1) * 128], rhs=h3[:, k, :],
                            start=(k == 0), stop=(k == 7))), reads=['gwin_%d' % k, '%s_%d' % (GH, k)], writes=[PS[pi]])
                    if on_dve:
                        P.op('dve', (lambda ps=ps, h=h, dst3=dst3, scl=scl: nc.vector.tensor_scalar(
                            out=dst3[:, h, :], in0=ps[:, :Wt], scalar1=scl, scalar2=None, op0=ALU.mult)),
                            reads=[PS[pi]], writes=['%s_%d' % (dn, h)])
                    else:
                        P.op('act', (lambda ps=ps, h=h, dst3=dst3, scl=scl: nc.scalar.mul(out=dst3[:, h, :], in_=ps[:, :Wt], mul=scl)),
                             reads=[PS[pi]], writes=['%s_%d' % (dn, h)])
            pi = pbank()
            ps = psum[pi]
            for k in range(8):
                P.op('pe', (lambda ps=ps, k=k: nc.tensor.matmul(
                    out=ps[0:16, :Wt], lhsT=win3[:, k, gcol:gcol + 16], rhs=h3[:, k, :], start=(k == 0), stop=(k == 7))),
                    reads=['gwin_%d' % k, '%s_%d' % (GH, k)], writes=[PS[pi]])
            P.op('dve', (lambda ps=ps: nc.vector.tensor_copy(out=gfs[0:16, :Wt], in_=ps[0:16, :Wt])), reads=[PS[pi]], writes=['gfs'])

        def stage_b(ti):
            c0, Wt, j = tiles[ti]
            bb = ti % 2
            need_out = (j == 0) or need_ctx
            x3 = v8(xts[bb])
            h3 = v8(hbs[bb])
            GH = 'gh%d' % bb
            GX = 'gxt%d' % bb
            if ti not in pre:
                proj_qk(ti, False)
            for h in range(4):
                pi = pbank()
                ps = psum[pi]
                P.op('pe', (lambda ps=ps, h=h: nc.tensor.matmul(
                    out=ps[:, :Wt], lhsT=wup[0:16, h * 128:(h + 1) * 128], rhs=gfs[0:16, :Wt], start=True, stop=True)),
                    reads=['gwup', 'gfs'], writes=[PS[pi]])
                P.op('act', (lambda ps=ps, h=h: nc.scalar.activation(out=E3v[:, h, :], in_=ps[:, :Wt], func=AF.Exp,
                                                                      scale=-1.0, bias=negb[:, h:h + 1])),
                     reads=[PS[pi], 'negb'], writes=['E'])
            P.op('act', lambda: nc.scalar.activation(out=spb, in_=Eb, func=AF.Ln, bias=1.0, scale=1.0), reads=['E'], writes=['sp'])
            for h in range(4):
                if bwd:
                    P.op('dve', (lambda h=h: nc.vector.tensor_tensor_scan(
                        out=B3[:, h, ::-1], data0=CST[:, maskc:maskc + Wt][:, ::-1], data1=sp3[:, h, ::-1], initial=0.0,
                        op0=ALU.mult, op1=ALU.add)), reads=['sp', 'CST'], writes=['B'])
                else:
                    P.op('dve', (lambda h=h: nc.vector.tensor_tensor_scan(
                        out=B3[:, h, :], data0=CST[:, maskc:maskc + Wt], data1=sp3[:, h, :], initial=0.0,
                        op0=ALU.mult, op1=ALU.add)), reads=['sp', 'CST'], writes=['B'])
            eidx = 0 if bwd else GCK - 1
            bend = B4[:, :, :, eidx]
            P.op('act', lambda: nc.scalar.activation(out=Eb, in_=Bb, func=AF.Exp, scale=-1.0 / 16), reads=['B'], writes=['E'])
            P.op('dve', lambda: nc.vector.tensor_tensor(out=qt, in0=q32, in1=Eb, op=ALU.mult),
                 reads=['E'] + ['q32_%d' % h for h in range(4)], writes=['qt'])
            P.op('act', lambda: nc.scalar.activation(out=Eb, in_=Bb, func=AF.Exp, scale=1.0 / 16), reads=['B', 'qt'], writes=['E'])
            P.op('dve', lambda: nc.vector.tensor_tensor(out=kt, in0=k32, in1=Eb, op=ALU.mult),
                 reads=['E'] + ['k32_%d' % h for h in range(4)], writes=['kt'])
            P.op('act', lambda: nc.scalar.activation(out=E33, in_=bend, func=AF.Exp, scale=-1.0 / 16), reads=['B'], writes=['E3'])
            P.op('dve', lambda: nc.vector.tensor_tensor(out=E4, in0=B4, in1=bend.unsqueeze(3).broadcast_to([128, 4, NCK, GCK]),
                                                        op=ALU.subtract), reads=['B', 'kt'], writes=['E'])
            P.op('act', lambda: nc.scalar.activation(out=Eb, in_=Eb, func=AF.Exp, scale=1.0 / 16), reads=['E'], writes=['E'])
            P.op('dve', lambda: nc.vector.tensor_tensor(out=kpT, in0=k32, in1=Eb, op=ALU.mult),
                 reads=['E'] + ['k32_%d' % h for h in range(4)], writes=['kpT'])
            vbanks = (3, 5, 6, 7)
            for blk in range(2):
                for half in range(2):
                    pi = vbanks[blk * 2 + half]
                    ps = psum[pi]
                    for k in range(8):
                        P.op('pe', (lambda ps=ps, k=k, blk=blk, half=half: nc.tensor.matmul(
                            out=ps[:, :512], lhsT=h3[:, k, blk * 128:(blk + 1) * 128],
                            rhs=win3[:, k, 1024 + half * 512:1024 + (half + 1) * 512], start=(k == 0), stop=(k == 7))),
                            reads=['gwin_%d' % k, '%s_%d' % (GH, k)], writes=[PS[pi]])
            if bwd and need_out:
                rbanks = (1, 2, 4, 0)
                for rg in range(4):
                    pi = rbanks[rg]
                    ps = psum[pi]
                    for r2 in range(2):
                        rc = rg * 2 + r2
                        for k in range(8):
                            P.op('pe', (lambda ps=ps, k=k, rc=rc, r2=r2: nc.tensor.matmul(
                                out=ps[:, r2 * 256:r2 * 256 + Wt], lhsT=win3[:, k, 2048 + rc * 128:2048 + (rc + 1) * 128],
                                rhs=h3[:, k, :], start=(k == 0), stop=(k == 7))),
                                reads=['gwin_%d' % k, '%s_%d' % (GH, k)], writes=[PS[pi]])
            for blk in range(2):
                for half in range(2):
                    pi = vbanks[blk * 2 + half]
                    ps = psum[pi]
                    if half == 0:
                        P.op('dve', (lambda ps=ps, blk=blk, half=half: nc.vector.tensor_copy(
                            out=vtok3[:, blk, half * 512:(half + 1) * 512], in_=ps[:, :512])),
                            reads=[PS[pi]], writes=['vtok_%d_%d' % (blk, half)])
                    else:
                        P.op('act', (lambda ps=ps, blk=blk, half=half: nc.scalar.copy(
                            out=vtok3[:, blk, half * 512:(half + 1) * 512], in_=ps[:, :512])),
                            reads=[PS[pi]], writes=['vtok_%d_%d' % (blk, half)])
            if bwd and need_out:
                for rg in range(4):
                    pi = rbanks[rg]
                    P.op('act', (lambda pi=pi, rg=rg: nc.scalar.activation(
                        out=rs3[:, rg * 2:rg * 2 + 2, :], in_=psum[pi][:, 0:512].rearrange("p (c w) -> p c w", c=2),
                        func=AF.Silu)), reads=[PS[pi]], writes=['rs_%d' % (rg * 2), 'rs_%d' % (rg * 2 + 1)])
            for blk in range(2):
                pti = pbank()
                psT = psum[pti][:, :].bitcast(BF16)
                for h in range(4):
                    P.op('pe', (lambda h=h, blk=blk, psT=psT: nc.tensor.transpose(
                        out=psT[:, h * 128:(h + 1) * 128], in_=kp3[:, h, blk * 128:(blk + 1) * 128], identity=ident_bf)),
                        reads=['kpT', 'ident'], writes=[PS[pti]])
                P.op('act', (lambda blk=blk, psT=psT: nc.scalar.copy(out=kptok3[:, blk, :], in_=psT[:, 0:512])),
                     reads=[PS[pti]], writes=['kptok_%d' % blk])
            if ti + 1 < nt:
                stage_a(ti + 1)
            blk_order = [1, 0] if bwd else [0, 1]
            if need_out:
                for blk in blk_order:
                    ai = blk
                    Aa = Ab[ai]
                    sci = pbank()
                    for h in range(4):
                        P.op('pe', (lambda blk=blk, h=h, sci=sci: nc.tensor.matmul(
                            out=psum[sci][:, h * 128:(h + 1) * 128],
                            lhsT=kt3[:, h, blk * 128:(blk + 1) * 128], rhs=qt3[:, h, blk * 128:(blk + 1) * 128],
                            start=True, stop=True)), reads=['kt', 'qt'], writes=[PS[sci]])
                    P.op('dve', (lambda Aa=Aa, sci=sci: nc.vector.tensor_tensor(
                        out=Aa.rearrange("p (h t) -> p h t", h=4), in0=psum[sci][:, 0:512].rearrange("p (h t) -> p h t", h=4),
                        in1=tri, op=ALU.mult)), reads=[PS[sci], 'CST'], writes=['A%d' % ai])
            pso = (4, 0)
            for blk in blk_order:
                ai = blk
                Aa = Ab[ai]
                ci = blk
                for h in range(4):
                    pi = (3, 5, 6, 7)[h]
                    P.op('pe', (lambda h=h, pi=pi, blk=blk: nc.tensor.matmul(
                        out=psum[pi][:, 0:256], lhsT=kptok3[:, blk, h * 128:(h + 1) * 128],
                        rhs=vtok3[:, blk, h * 256:(h + 1) * 256], start=True, stop=True)),
                        reads=['kptok_%d' % blk, 'vtok_%d_%d' % (blk, h // 2)], writes=[PS[pi]])
                if need_out:
                    for h in range(4):
                        for vc in range(2):
                            oc = h * 2 + vc
                            po_i = pso[oc // 4]
                            osl = slice((oc % 4) * 128, (oc % 4 + 1) * 128)
                            P.op('pe', (lambda h=h, vc=vc, po_i=po_i, osl=osl, blk=blk, Aa=Aa: nc.tensor.matmul(
                                out=psum[po_i][:, osl],
                                lhsT=vtok3[:, blk, h * 256 + vc * 128:h * 256 + (vc + 1) * 128],
                                rhs=Aa[:, h * 128:(h + 1) * 128], start=True, stop=False)),
                                reads=['vtok_%d_%d' % (blk, h // 2), 'A%d' % ai], writes=[PS[po_i]])
                            P.op('pe', (lambda h=h, vc=vc, po_i=po_i, osl=osl, blk=blk: nc.tensor.matmul(
                                out=psum[po_i][:, osl], lhsT=Sb3[:, h, vc * 128:(vc + 1) * 128],
                                rhs=qt3[:, h, blk * 128:(blk + 1) * 128], start=False, stop=True)),
                                reads=['Sbf_%d' % h, 'qt'], writes=[PS[po_i]])
                    for g2 in range(2):
                        po_i = pso[g2]
                        P.op('act', (lambda g2=g2, po_i=po_i, blk=blk: nc.scalar.copy(
                            out=ot3[:, g2 * 4:(g2 + 1) * 4, blk * 128:(blk + 1) * 128],
                            in_=psum[po_i][:, 0:512].rearrange("p (c t) -> p c t", c=4))),
                            reads=[PS[po_i]], writes=['ot'])
                for h in range(4):
                    pi = (3, 5, 6, 7)[h]
                    P.op('dve', (lambda h=h, pi=pi, ci=ci: nc.vector.scalar_tensor_tensor(
                        out=S3[:, h, :], in0=S3[:, h, :], scalar=E3[:, h * NCK + ci:h * NCK + ci + 1],
                        in1=psum[pi][:, 0:256], op0=ALU.mult, op1=ALU.add)),
                        reads=[PS[pi], 'E3', 'S32_%d' % h], writes=['S32_%d' % h])
                    P.op('act', (lambda h=h: nc.scalar.copy(out=Sb3[:, h, :], in_=S3[:, h, :])),
                         reads=['S32_%d' % h], writes=['Sbf_%d' % h])
            if not need_out:
                return
            if not bwd:
                dst = of_d[:, c0:c0 + Wt].rearrange("(c p) w -> p c w", p=128)
                P.op('sp', (lambda dst=dst: nc.sync.dma_start(out=dst, in_=ot3)), reads=['ot'], writes=blks('of', c0, Wt),
                     lane='d_of')
                return
            ofl = ofls[bb]
            P.op('dve', lambda: nc.vector.tensor_tensor(out=ot, in0=ot, in1=ofl, op=ALU.add), reads=['ot', 'ofl%d' % bb], writes=['ot'])
            P.op('act', lambda: nc.scalar.activation(out=sq8, in_=ot, func=AF.Square), reads=['ot'], writes=['sq8'])
            if ti + 1 < nt:
                proj_qk(ti + 1, True)
                pre.add(ti + 1)
            nb_ = [pbank(), pbank()]
            for h in range(4):
                pi = nb_[h // 2]
                for vc in range(2):
                    oc = h * 2 + vc
                    P.op('pe', (lambda pi=pi, h=h, vc=vc, oc=oc: nc.tensor.matmul(
                        out=psum[pi][:, (h % 2) * 256:(h % 2) * 256 + Wt], lhsT=ones_bf, rhs=sq83[:, oc, :],
                        start=(vc == 0), stop=(vc == 1))), reads=['sq8', 'ones'], writes=[PS[pi]])
            for hh in range(2):
                pi = nb_[hh]
                P.op('act', (lambda pi=pi, hh=hh: nc.scalar.activation(out=Eb[:, hh * 512:(hh + 1) * 512], in_=psum[pi][:, 0:512],
                                                                        func=AF.Ln, scale=1.0 / 256, bias=eps_ap)),
                     reads=[PS[pi], 'ones'], writes=['E'])
            P.op('act', lambda: nc.scalar.activation(out=Bb, in_=Eb, func=AF.Exp, scale=-0.5), reads=['E'], writes=['B'])
            for h in range(4):
                for vc in range(2):
                    oc = h * 2 + vc
                    P.op('dve', (lambda oc=oc, vc=vc, h=h: nc.vector.scalar_tensor_tensor(
                        out=ot3[:, oc, :], in0=ot3[:, oc, :], scalar=V[:, ngo + vc:ngo + vc + 1], in1=B3[:, h, :],
                        op0=ALU.mult, op1=ALU.mult)), reads=['ot', 'B', 'V'], writes=['ot'])
            P.op('dve', lambda: nc.vector.tensor_tensor(out=yb, in0=ot, in1=rsb, op=ALU.mult),
                 reads=['ot'] + ['rs_%d' % rc for rc in range(8)], writes=['gy_%d' % c for c in range(8)])
            outproj(y3, 'gy', wo3, 'gwo', x3, GX, l, j, Wt, [1, 2])
            store_x(x3, GX, c0, Wt, 'd_gxo')

        stage_a(0)
        for ti in range(nt):
            stage_b(ti)

    def attn_layer(l, src_ap, src_name, need_ctx):
        m0 = A.mark()
        W = 512
        win3 = A.bf16(8 * 1536).rearrange("p (k n) -> p k n", k=8)
        wload(win3, attn_w_in[0], 8, 'awin', 0, 1536)
        wo3 = A.bf16(8 * D).rearrange("p (k n) -> p k n", k=8)
        wload(wo3, attn_w_o[0], 8, 'awo', 0, D)
        KT3 = A.bf16(2 * T).rearrange("p (g t) -> p g t", g=2)
        Vt3 = A.bf16(34 * 256).rearrange("p (b n) -> p b n", b=34)
        rot_f = CST[:, C_ROT:C_ROT + 128]
        cosb, sinb = A.f32(W), A.f32(W)
        xt = A.f32(8 * W)
        hb = A.bf16(8 * W)
        xt2 = [xt, A.f32(8 * W)]
        hb2 = [hb, A.bf16(8 * W)]
        knb2 = [A.f32(W) for _ in range(2)]
        t1b2 = [A.f32(W) for _ in range(2)]
        t2b2 = [A.f32(W) for _ in range(2)]
        lnb2 = [A.f32(W) for _ in range(2)]
        qh = A.bf16(8 * W)
        yb = A.bf16(8 * W)
        Pt = [A.bf16(W) for _ in range(3)]
        rden = A.f32(W)
        dacc = [A.f32(W) for _ in range(2)]
        st = {'sb': 0, 'pt': 0, 'hd': 0}
        SC = 128.0 ** -0.5

        def sbank():
            pi = 2 + st['sb'] % 2
            st['sb'] += 1
            return pi

        def load_rope(c0, Wt):
            for tb, idx, nm in ((cosb, 0, 'cos'), (sinb, 1, 'sin')):
                src = rope_d[idx, :, c0 - CTX:c0 - CTX + Wt]
                P.op('sp', (lambda tb=tb, src=src: nc.sync.dma_start(out=tb[:, :Wt], in_=src)), writes=[nm], lane='d_' + nm)

        qn = {'n': 0, 'pj': 0}

        def qk_norm_rope(ps, psn, gname, dst, dstn, Wt, is_lat):
            i = cnt['sq'] % 2
            cnt['sq'] += 1
            sb = sqb[i]
            k2 = qn['n'] % 2
            qn['n'] += 1
            kn_, t1_, t2_, ln_ = knb2[k2], t1b2[k2], t2b2[k2], lnb2[k2]
            P.op('act', lambda: nc.scalar.activation(out=sb[:, :Wt], in_=ps[:, :Wt], func=AF.Square), reads=[psn], writes=['sq%d' % i])
            pq = sbank()
            P.op('pe', lambda: nc.tensor.matmul(out=psum[pq][:, :Wt], lhsT=ones_bf, rhs=sb[:, :Wt], start=True, stop=True),
                 reads=['sq%d' % i, 'ones'], writes=[PS[pq]])
            P.op('act', lambda: nc.scalar.activation(out=ln_[:, :Wt], in_=psum[pq][:, :Wt], func=AF.Ln, scale=1.0 / 128, bias=eps_ap),
                 reads=[PS[pq], 'ones'], writes=['ln%d' % k2])
            ri = cnt['rstd'] % 2
            cnt['rstd'] += 1
            rs_ = rstdb[ri]
            P.op('act', lambda: nc.scalar.activation(out=rs_[:, :Wt], in_=ln_[:, :Wt], func=AF.Exp, scale=-0.5),
                 reads=['ln%d' % k2], writes=['rstd%d' % ri])
            g = V[:, VEC_OFF[gname][0]:VEC_OFF[gname][0] + 1]
            if not is_lat:
                P.op('dve', lambda: nc.vector.scalar_tensor_tensor(out=dst, in0=ps[:, :Wt], scalar=g, in1=rs_[:, :Wt],
                                                                   op0=ALU.mult, op1=ALU.mult),
                     reads=[psn, 'rstd%d' % ri, 'V'], writes=[dstn])
                return
            P.op('dve', lambda: nc.vector.scalar_tensor_tensor(out=kn_[:, :Wt], in0=ps[:, :Wt], scalar=g, in1=rs_[:, :Wt],
                                                               op0=ALU.mult, op1=ALU.mult),
                 reads=[psn, 'rstd%d' % ri, 'V'], writes=['kn%d' % k2])
            pr = sbank()
            P.op('pe', lambda: nc.tensor.matmul(out=psum[pr][:, :Wt], lhsT=rot_f, rhs=kn_[:, :Wt], start=True, stop=True),
                 reads=['kn%d' % k2, 'CST'], writes=[PS[pr]])
            P.op('dve', lambda: nc.vector.tensor_tensor(out=t1_[:, :Wt], in0=kn_[:, :Wt], in1=cosb[:, :Wt], op=ALU.mult),
                 reads=['kn%d' % k2, 'cos'], writes=['t1%d' % k2])
            P.op('dve', lambda: nc.vector.tensor_tensor(out=t2_[:, :Wt], in0=psum[pr][:, :Wt], in1=sinb[:, :Wt], op=ALU.mult),
                 reads=[PS[pr], 'sin'], writes=['t2%d' % k2])
            P.op('dve', lambda: nc.vector.tensor_tensor(out=dst, in0=t1_[:, :Wt], in1=t2_[:, :Wt], op=ALU.add),
                 reads=['t1%d' % k2, 't2%d' % k2], writes=[dstn])

        def proj(h3, col, Wt, hname='ah'):
            pj = qn['pj'] % 2
            qn['pj'] += 1
            ps = psum[pj]
            for k in range(8):
                P.op('pe', (lambda k=k: nc.tensor.matmul(out=ps[:, :Wt], lhsT=win3[:, k, col:col + 128], rhs=h3[:, k, :],
                                                         start=(k == 0), stop=(k == 7))),
                     reads=['awin_%d' % k, '%s_%d' % (hname, k)], writes=[PS[pj]])
            return ps, PS[pj]

        def pass1(c0, Wt, j):
            load_x(xt, 'axt', c0, Wt, src_ap, src_name, 'd_axt')
            if j == 0:
                load_rope(c0, Wt)
            norm_stage(xt, 'axt', Wt, 0, hb, 'ah', lambda c: gm_ap(l, 0, c, j), lambda c: shift_ap(l, 0, c, j))
            h3 = hb[:, :8 * Wt].rearrange("p (c w) -> p c w", c=8)
            for g in range(2):
                ps, psn = proj(h3, 1024 + g * 128, Wt)
                qk_norm_rope(ps, psn, 'ak_g', KT3[:, g, c0:c0 + Wt], 'KT_%d_%d' % (g, c0), Wt, j == 0)
            for blk in range(Wt // 128):
                pi = sbank()
                ps = psum[pi]
                gb = c0 // 128 + blk
                for k in range(8):
                    P.op('pe', (lambda ps=ps, k=k, blk=blk: nc.tensor.matmul(
                        out=ps[:, :256], lhsT=h3[:, k, blk * 128:(blk + 1) * 128], rhs=win3[:, k, 1280:1536],
                        start=(k == 0), stop=(k == 7))), reads=['awin_%d' % k, 'ah_%d' % k], writes=[PS[pi]])
                P.op('act', (lambda ps=ps, gb=gb: nc.scalar.copy(out=Vt3[:, gb, :], in_=ps[:, :256])),
                     reads=[PS[pi]], writes=['Vt_%d' % gb])

        def p2_load(qi):
            c0, Wt, j = qtiles[qi]
            b = qi % 2
            load_x(xt2[b], ('axt', 'axq1')[b], c0, Wt, src_ap, src_name, ('d_axt', 'd_axq1')[b])

        def p2_norm(qi):
            c0, Wt, j = qtiles[qi]
            b = qi % 2
            norm_stage(xt2[b], ('axt', 'axq1')[b], Wt, 0, hb2[b], ('ah', 'ahq1')[b],
                       lambda c: gm_ap(l, 0, c, j), lambda c: shift_ap(l, 0, c, j))

        def pass2(qi):
            c0, Wt, j = qtiles[qi]
            b = qi % 2
            AX, AH = ('axt', 'axq1')[b], ('ah', 'ahq1')[b]
            x3 = xt2[b][:, :8 * Wt].rearrange("p (c w) -> p c w", c=8)
            if qi + 1 < len(qtiles):
                p2_load(qi + 1)
            if j == 0:
                load_rope(c0, Wt)
            h3 = hb2[b][:, :8 * Wt].rearrange("p (c w) -> p c w", c=8)
            q3 = qh[:, :8 * Wt].rearrange("p (c w) -> p c w", c=8)
            y3 = yb[:, :8 * Wt].rearrange("p (c w) -> p c w", c=8)
            for hd in range(8):
                ps, psn = proj(h3, hd * 128, Wt, AH)
                qk_norm_rope(ps, psn, 'aq_g', q3[:, hd, :], 'qh_%d' % hd, Wt, j == 0)
            ktiles = list(range(2)) if j == 1 else list(range(34))
            nk = len(ktiles)
            def do_head(hd):
                g = hd // 4
                po_i = 4 + st['hd'] % 2
                pd_i = 6 + st['hd'] % 2
                st['hd'] += 1
                po, pd = psum[po_i], psum[pd_i]

                def kread(kt):
                    c = kt * 128
                    if c < CTX:
                        return 'KT_%d_%d' % (g, 0)
                    return 'KT_%d_%d' % (g, CTX + ((c - CTX) // 512) * 512)

                def s_mm(kt):
                    pi = sbank()
                    P.op('pe', (lambda pi=pi, kt=kt: nc.tensor.matmul(
                        out=psum[pi][:, :Wt], lhsT=KT3[:, g, kt * 128:(kt + 1) * 128], rhs=q3[:, hd, :],
                        start=True, stop=True)), reads=[kread(kt), 'qh_%d' % hd], writes=[PS[pi]])
                    return pi

                cur_s = s_mm(ktiles[0])
                for ii, kt in enumerate(ktiles):
                    nxt_s = s_mm(ktiles[ii + 1]) if ii + 1 < nk else None
                    pti = st['pt'] % 3
                    st['pt'] += 1
                    pt = Pt[pti]
                    P.op('act', (lambda cur_s=cur_s, pt=pt: nc.scalar.activation(
                        out=pt[:, :Wt], in_=psum[cur_s][:, :Wt], func=AF.Exp, scale=SC)),
                        reads=[PS[cur_s]], writes=['Pt%d' % pti])
                    P.op('pe', (lambda kt=kt, pt=pt, ii=ii: nc.tensor.matmul(
                        out=po[:, :Wt], lhsT=Vt3[:, kt, g * 128:(g + 1) * 128], rhs=pt[:, :Wt],
                        start=(ii == 0), stop=(ii == nk - 1))), reads=['Vt_%d' % kt, 'Pt%d' % pti], writes=[PS[po_i]])
                    da = dacc[ii % 2]
                    dn = 'dacc%d' % (ii % 2)
                    if ii < 2:
                        P.op('dve', (lambda da=da, pt=pt: nc.vector.tensor_copy(out=da[:, :Wt], in_=pt[:, :Wt])),
                             reads=['Pt%d' % pti], writes=[dn])
                    else:
                        P.op('dve', (lambda da=da, pt=pt: nc.vector.tensor_tensor(out=da[:, :Wt], in0=da[:, :Wt], in1=pt[:, :Wt],
                                                                                 op=ALU.add)),
                             reads=['Pt%d' % pti, dn], writes=[dn])
                    cur_s = nxt_s
                P.op('dve', lambda: nc.vector.tensor_tensor(out=dacc[0][:, :Wt], in0=dacc[0][:, :Wt], in1=dacc[1][:, :Wt], op=ALU.add),
                     reads=['dacc0', 'dacc1'], writes=['dacc0'])
                P.op('pe', lambda: nc.tensor.matmul(out=pd[:, :Wt], lhsT=ones_f, rhs=dacc[0][:, :Wt], start=True, stop=True),
                     reads=['ones', 'dacc0'], writes=[PS[pd_i]])
                P.op('dve', lambda: nc.vector.reciprocal(out=rden[:, :Wt], in_=pd[:, :Wt]), reads=[PS[pd_i]], writes=['rden'])
                P.op('dve', (lambda hd=hd: nc.vector.tensor_tensor(out=y3[:, hd, :], in0=po[:, :Wt], in1=rden[:, :Wt], op=ALU.mult)),
                     reads=[PS[po_i], 'rden'], writes=['ay_%d' % hd])

            for hd in range(8):
                do_head(hd)
                if hd == 3 and qi + 1 < len(qtiles):
                    p2_norm(qi + 1)
            outproj(y3, 'ay', wo3, 'awo', x3, AX, l, j, Wt, [0, 1])
            store_x(x3, AX, c0, Wt, 'd_axo')

        tiles = seq_tiles(W)
        for (c0, Wt, j) in tiles:
            pass1(c0, Wt, j)
        qtiles = [t_ for t_ in tiles if not (t_[2] == 1 and not need_ctx)]
        p2_load(0)
        p2_norm(0)
        for qi in range(len(qtiles)):
            pass2(qi)
        P.barrier()
        A.reset(m0)

    outbufs = []
    for li, l in enumerate(layers):
        last = do_final and (li == len(layers) - 1)
        if do_mixer:
            kind = l % 3
            need_ctx = l < NL - 1
            if kind == 0:
                gla_layer(l, l // 3, cur['ap'], cur['name'], need_ctx)
            if kind == 1:
                lru_layer(l, cur['ap'], cur['name'])
            if kind == 2:
                attn_layer(l, cur['ap'], cur['name'], need_ctx)
            cur = {'ap': xs, 'name': 'xs'}
        if cfg.get('mlp', True):
            mlp_sweep(l, last, cur['ap'], cur['name'])
            cur = {'ap': xs, 'name': 'xs'}

    if not do_final:
        for i in range(17):
            P.op('sp', (lambda i=i: nc.sync.dma_start(out=out_d[:, i * 256:(i + 1) * 256],
                                                       in_=xs[:, i * 256:(i + 1) * 256])),
                 reads=['xs_%d' % i], writes=['out_%d' % i], lane='d_cp%d' % (i % 4))
            outbufs.append('out_%d' % i)
    P.op('sp', None, reads=outbufs)
    P.emit(nc)
    es.close()
    return nc


_CACHE = {}


def make_in_maps(inputs):
    x = np.asarray(inputs['x'], np.float32)
    ctx = np.asarray(inputs['ctx'], np.float32)
    c = np.asarray(inputs['c'], np.float32)
    c_ctx = np.asarray(inputs['c_ctx'], np.float32)
    vecs = pack_vecs(inputs)
    consts = make_consts()
    rope = make_rope()
    maps = []
    for b in range(x.shape[0]):
        xT = np.ascontiguousarray(np.concatenate([ctx[b], x[b]], axis=0).T)
        cc = np.stack([c[b], c_ctx], axis=1)
        c2 = np.ascontiguousarray(cc.reshape(8, 128, 2).transpose(1, 0, 2).reshape(128, 16))
        maps.append({
            'xT': xT, 'c2': c2, 'vecs': vecs,
            'w_mod': np.asarray(inputs['w_mod'], np.float32),
            'w_mlp1': np.asarray(inputs['w_mlp1'], np.float32),
            'w_mlp2': np.asarray(inputs['w_mlp2'], np.float32),
            'consts': consts, 'rope': rope,
            **{k: np.asarray(inputs[k], np.float32) for k in (
                'gla_w_in', 'gla_w_up_f', 'gla_w_up_b', 'gla_w_o', 'lru_w_in', 'lru_wa_f', 'lru_wx_f',
                'lru_wa_b', 'lru_wx_b', 'lru_w_o', 'attn_w_in', 'attn_w_o')},
        })
    return maps


def run(inputs, cfg, cores=8):
    key = repr(sorted(cfg.items()))
    if key not in _CACHE:
        _CACHE[key] = build(cfg)
    nc = _CACHE[key]
    maps = make_in_maps(inputs)[:cores]
    res = run_bass_kernel_spmd(nc, maps, core_ids=list(range(cores)))
    outs = [np.asarray(r['outT']).T for r in res.results]
    return np.stack(outs, axis=0)


def kernel(**inputs):
    return run(inputs, {}).astype(np.float32)
```

```python
import contextlib
import numpy as np
import concourse.bass as bass
import concourse.mybir as mybir
from concourse.bass_utils import run_bass_kernel_spmd

F32 = mybir.dt.float32
BF16 = mybir.dt.bfloat16
AF = mybir.ActivationFunctionType
ALU = mybir.AluOpType

D = 1024
NCH = 8
CTX = 256
SEQ = 4096
T = CTX + SEQ
DFF = 4096
NL = 4
EPS = 1e-6

ENGS = ('pe', 'act', 'dve', 'pool', 'sp')


class Prog:
    def __init__(self):
        self.ops = []
        self.eng_ops = {e: [] for e in ENGS}
        self.lane_ops = {}
        self.last_w = {}
        self.readers = {}

    def op(self, eng, fn, reads=(), writes=(), lane=None):
        oid = len(self.ops)
        deps = set()
        for b in reads:
            w = self.last_w.get(b)
            if w is not None:
                deps.add(w)
        for b in writes:
            w = self.last_w.get(b)
            if w is not None:
                deps.add(w)
            for r in self.readers.get(b, {}).values():
                deps.add(r)
        ln = lane if lane is not None else eng
        lst = self.lane_ops.setdefault(ln, [])
        pos = len(lst)
        lst.append(oid)
        self.ops.append(dict(eng=eng, fn=fn, deps=deps, lane=ln, pos=pos,
                             dma=lane is not None, signal=lane is not None))
        self.eng_ops[eng].append(oid)
        for b in writes:
            self.last_w[b] = oid
            self.readers[b] = {}
        for b in reads:
            self.readers.setdefault(b, {})[ln] = oid
        return oid

    def barrier(self):
        lasts = set()
        for lst in self.lane_ops.values():
            for oid in reversed(lst):
                if self.ops[oid]['fn'] is not None:
                    lasts.add(oid)
                    break
        for e in ENGS:
            oid = self.op(e, None)
            self.ops[oid]['deps'] |= lasts

    def resolve(self):
        ops = self.ops
        needed = set()
        for o in ops:
            needed.update(o['deps'])
        clock = {e: {} for e in ENGS}
        snap = {}

        def merge(a, b):
            out = None
            for k, v in b.items():
                if a.get(k, -1) < v:
                    if out is None:
                        out = dict(a)
                    out[k] = v
            return a if out is None else out

        for oid, o in enumerate(ops):
            E = o['eng']
            ck = clock[E]
            waits = []
            for d in sorted(o['deps'], reverse=True):
                do = ops[d]
                if do['eng'] == E and not do['dma'] and E == 'pe':
                    continue
                if ck.get(do['lane'], -1) >= do['pos']:
                    continue
                waits.append(d)
                ck = merge(ck, snap[d])
            if o['dma'] and o['pos'] > 0 and ck.get(o['lane'], -1) < o['pos'] - 1:
                prev = self.lane_ops[o['lane']][o['pos'] - 1]
                waits.append(prev)
                ck = merge(ck, snap[prev])
            clock[E] = ck
            o['waits'] = waits
            for w in waits:
                ops[w]['signal'] = True
            if oid in needed or o['dma']:
                s = dict(ck)
                if s.get(o['lane'], -1) < o['pos']:
                    s[o['lane']] = o['pos']
                snap[oid] = s
        for ln, lst in self.lane_ops.items():
            cnt = 0
            for oid in lst:
                o = ops[oid]
                if o['signal']:
                    cnt += 16 if o['dma'] else 1
                o['semval'] = cnt

    def emit(self, nc):
        self.resolve()
        ops = self.ops
        with contextlib.ExitStack() as es:
            sems = {}
            for ln in self.lane_ops:
                sems[ln] = es.enter_context(nc.semaphore("s_" + ln))
            block = es.enter_context(nc.Block())

            def run(eng_name, eng):
                for oid in self.eng_ops[eng_name]:
                    o = ops[oid]
                    for w in o['waits']:
                        wo = ops[w]
                        eng.wait_ge(sems[wo['lane']], wo['semval'])
                    if o['fn'] is None:
                        continue
                    ins = o['fn']()
                    if o['signal']:
                        ins.then_inc(sems[o['lane']], 16 if o['dma'] else 1)

            @block.tensor
            def _(e):
                run('pe', e)

            @block.scalar
            def _(e):
                run('act', e)

            @block.vector
            def _(e):
                run('dve', e)

            @block.gpsimd
            def _(e):
                run('pool', e)

            @block.sync
            def _(e):
                run('sp', e)


def _fm(v):
    v = np.asarray(v, np.float32)
    lead = int(np.prod(v.shape[:-1])) if v.ndim > 1 else 1
    n = v.shape[-1] // 128
    return np.ascontiguousarray(v.reshape(lead, n, 128).transpose(2, 0, 1).reshape(128, lead * n))


VEC_SPEC = [
    ('nmix', NL * 8), ('nmlp', NL * 8), ('bmod', NL * 48), ('fin', 8),
    ('gla_bf', 2 * 4), ('gla_bb', 2 * 4), ('gla_ng', 2 * 2),
    ('lru_cw', 4 * 8), ('lru_cb', 8), ('ba_f', 8), ('bx_f', 8), ('lam_f', 8),
    ('ba_b', 8), ('bx_b', 8), ('lam_b', 8), ('aq_g', 1), ('ak_g', 1),
]
VEC_OFF = {}
_o = 0
for _n, _w in VEC_SPEC:
    VEC_OFF[_n] = (_o, _w)
    _o += _w
NV = _o


C_IDENT = 0
C_ROT = 128
C_TRIF = 256
C_TRIB = 384
C_MASKF = 512
C_MASKB = 1024
NCONST = 1536
GCK = 128


def make_consts():
    c = np.zeros((128, NCONST), np.float32)
    c[:, C_IDENT:C_IDENT + 128] = np.eye(128, dtype=np.float32)
    rot = np.zeros((128, 128), np.float32)
    for m in range(64):
        rot[m + 64, m] = -1.0
    for m in range(64, 128):
        rot[m - 64, m] = 1.0
    c[:, C_ROT:C_ROT + 128] = rot
    p = np.arange(128)[:, None]
    t = np.arange(128)[None, :]
    c[:, C_TRIF:C_TRIF + 128] = (p <= t).astype(np.float32)
    c[:, C_TRIB:C_TRIB + 128] = (p >= t).astype(np.float32)
    tt = np.arange(512)
    c[:, C_MASKF:C_MASKF + 512] = (tt % GCK != 0).astype(np.float32)[None, :]
    c[:, C_MASKB:C_MASKB + 512] = (tt % GCK != GCK - 1).astype(np.float32)[None, :]
    return c


def make_rope():
    tpos = np.arange(SEQ)
    row = (tpos // 64).astype(np.float32)
    col = (tpos % 64).astype(np.float32)
    inv_freq = (10000.0 ** (-np.arange(32, dtype=np.float32) / 32)).astype(np.float32)
    ang = np.concatenate([row[:, None] * inv_freq, col[:, None] * inv_freq], axis=-1)
    ang = np.concatenate([ang, ang], axis=-1).T
    return np.stack([np.cos(ang), np.sin(ang)]).astype(np.float32)


def pack_vecs(inp):
    parts = {
        'nmix': inp['norm_mix_g'], 'nmlp': inp['norm_mlp_g'], 'bmod': inp['b_mod'], 'fin': inp['final_g'],
        'gla_bf': inp['gla_b_f'], 'gla_bb': inp['gla_b_b'], 'gla_ng': inp['gla_norm_g'],
        'lru_cw': inp['lru_conv_w'][0], 'lru_cb': inp['lru_conv_b'][0],
        'ba_f': inp['lru_ba_f'][0], 'bx_f': inp['lru_bx_f'][0], 'lam_f': inp['lru_lam_f'][0],
        'ba_b': inp['lru_ba_b'][0], 'bx_b': inp['lru_bx_b'][0], 'lam_b': inp['lru_lam_b'][0],
        'aq_g': inp['attn_q_g'][0], 'ak_g': inp['attn_k_g'][0],
    }
    out = np.zeros((128, NV), np.float32)
    for n, w in VEC_SPEC:
        o, _ = VEC_OFF[n]
        a = _fm(parts[n])
        assert a.shape == (128, w), (n, a.shape, w)
        out[:, o:o + w] = a
    return out


class Arena:
    def __init__(self, slab, nwords):
        self.slab = slab
        self.n = nwords
        self.top = 0

    def f32(self, words):
        a = self.slab[:, self.top:self.top + words]
        self.top += words
        assert self.top <= self.n, ("SBUF arena overflow", self.top, self.n)
        return a

    def bf16(self, elems):
        assert elems % 2 == 0
        return self.f32(elems // 2).bitcast(BF16)

    def mark(self):
        return self.top

    def reset(self, m):
        self.top = m


def build(cfg):
    layers = cfg.get('layers', list(range(NL)))
    do_mixer = cfg.get('mixer', True)
    do_final = cfg.get('final', True)
    nc = bass.Bass("TRN2", target_bir_lowering=False)
    P = Prog()

    def dram_in(name, shape):
        return nc.dram_tensor(name, list(shape), F32, kind="ExternalInput").ap()

    xin = dram_in("xT", (D, T))
    c2 = dram_in("c2", (128, 16))
    vecs_d = dram_in("vecs", (128, NV))
    w_mod = dram_in("w_mod", (NL, D, 6 * D))
    w_mlp1 = dram_in("w_mlp1", (NL, D, DFF))
    w_mlp2 = dram_in("w_mlp2", (NL, DFF, D))
    gla_w_in = dram_in("gla_w_in", (2, D, 3104))
    gla_w_up_f = dram_in("gla_w_up_f", (2, 16, 512))
    gla_w_up_b = dram_in("gla_w_up_b", (2, 16, 512))
    gla_w_o = dram_in("gla_w_o", (2, D, D))
    lru_w_in = dram_in("lru_w_in", (1, D, 2 * D))
    lru_wab = {('a', 0): dram_in("lru_wa_f", (1, 8, 128, 128)), ('x', 0): dram_in("lru_wx_f", (1, 8, 128, 128)),
               ('a', 1): dram_in("lru_wa_b", (1, 8, 128, 128)), ('x', 1): dram_in("lru_wx_b", (1, 8, 128, 128))}
    lru_w_o = dram_in("lru_w_o", (1, D, D))
    attn_w_in = dram_in("attn_w_in", (1, D, 1536))
    attn_w_o = dram_in("attn_w_o", (1, D, D))
    consts_d = dram_in("consts", (128, NCONST))
    rope_d = dram_in("rope", (2, 128, SEQ))
    OUTW = SEQ if do_final else T
    out_d = nc.dram_tensor("outT", [D, OUTW], F32, kind="ExternalOutput").ap()
    xs = nc.dram_tensor("xs", [D, T], F32, kind="Internal").ap()
    gx = nc.dram_tensor("gx", [2 * D, T], F32, kind="Internal").ap()
    of_d = nc.dram_tensor("of", [D, T], F32, kind="Internal").ap()

    AW = 53000
    es = contextlib.ExitStack()
    slab = es.enter_context(nc.sbuf_tensor("slab", [128, AW], F32))
    A = Arena(slab, AW)
    psum = [es.enter_context(nc.psum_tensor("ps%d" % i, [128, 512], F32)) for i in range(8)]
    PS = ['ps%d' % i for i in range(8)]

    V = A.f32(NV)
    mods = A.f32(NL * 48 * 2)
    gm = A.f32(NL * 2 * 8 * 2)
    sc = A.f32(16)
    ones_bf = A.bf16(128)
    ones_f = A.f32(128)
    CST = A.f32(NCONST)
    eps_t = A.f32(2)
    eps_ap = eps_t[:, 0:1]
    one_ap = ones_f[:, 0:1]
    ident_bf = A.bf16(128)
    mark0 = A.mark()
    P.op('sp', lambda: nc.sync.dma_start(out=CST, in_=consts_d), writes=['CST'], lane='d_cst')
    P.op('pool', lambda: nc.gpsimd.dma_start(out=ident_bf, in_=consts_d[:, C_IDENT:C_IDENT + 128]),
         writes=['ident'], lane='d_id')

    def vcol(name, i=0, n=1):
        o, w = VEC_OFF[name]
        return V[:, o + i:o + i + n]

    def mod_ap(l, which, c, j):
        k = (l * 48 + which * 8 + c) * 2 + j
        return mods[:, k:k + 1]

    def gm_ap(l, which, c, j):
        k = ((l * 2 + which) * 8 + c) * 2 + j
        return gm[:, k:k + 1]

    P.op('sp', lambda: nc.sync.dma_start(out=V, in_=vecs_d), writes=['V'], lane='d_V')
    P.op('sp', lambda: nc.sync.dma_start(out=sc, in_=c2), writes=['sc'], lane='d_sc')
    P.op('dve', lambda: nc.vector.memset(ones_bf, 1.0), writes=['ones'])
    P.op('dve', lambda: nc.vector.memset(ones_f, 1.0), writes=['ones'])
    P.op('dve', lambda: nc.vector.memset(eps_t, EPS), writes=['ones'])
    P.op('act', lambda: nc.scalar.activation(out=sc, in_=sc, func=AF.Silu), reads=['sc'], writes=['sc'])

    NB = 8
    BW = 6 * D // NB
    wm_bufs = [A.f32(8 * BW) for _ in range(2)]
    sc3 = sc.rearrange("p (k j) -> p k j", j=2)
    blk = 0
    for l in layers:
        for nb in range(NB):
            wb = wm_bufs[blk % 2]
            wbn = 'wm%d' % (blk % 2)
            wb3 = wb.rearrange("p (k n) -> p k n", k=8)
            src = w_mod[l, :, nb * BW:(nb + 1) * BW].rearrange("(k p) n -> p k n", p=128)
            P.op('sp', (lambda wb3=wb3, src=src: nc.sync.dma_start(out=wb3, in_=src)),
                 writes=[wbn], lane='d_' + wbn)
            pb = psum[blk % 2]
            pbn = PS[blk % 2]
            for fc in range(BW // 128):
                for k in range(8):
                    P.op('pe', (lambda pb=pb, wb3=wb3, fc=fc, k=k: nc.tensor.matmul(
                        out=pb[:, fc * 2:fc * 2 + 2], lhsT=wb3[:, k, fc * 128:(fc + 1) * 128],
                        rhs=sc3[:, k, :], start=(k == 0), stop=(k == 7))),
                        reads=[wbn, 'sc'], writes=[pbn])
            nfc = BW // 128
            c0 = nb * nfc
            mo = mods[:, (l * 48 + c0) * 2:(l * 48 + c0 + nfc) * 2].rearrange("p (c j) -> p c j", j=2)
            bo, _ = VEC_OFF['bmod']
            bm = V[:, bo + l * 48 + c0: bo + l * 48 + c0 + nfc].unsqueeze(2).broadcast_to([128, nfc, 2])
            P.op('dve', (lambda mo=mo, pb=pb, bm=bm, nfc=nfc: nc.vector.tensor_tensor(
                out=mo, in0=pb[:, 0:nfc * 2].rearrange("p (c j) -> p c j", j=2), in1=bm, op=ALU.add)),
                reads=[pbn, 'V'], writes=['mods'])
            blk += 1
        for which, gname in ((0, 'nmix'), (1, 'nmlp')):
            go, _ = VEC_OFF[gname]
            g8 = V[:, go + l * 8: go + l * 8 + 8].unsqueeze(2).broadcast_to([128, 8, 2])
            sc_ap = mods[:, (l * 48 + (which * 3 + 1) * 8) * 2:(l * 48 + (which * 3 + 2) * 8) * 2].rearrange(
                "p (c j) -> p c j", j=2)
            gmo = gm[:, ((l * 2 + which) * 8) * 2:((l * 2 + which) * 8 + 8) * 2].rearrange("p (c j) -> p c j", j=2)
            P.op('dve', (lambda gmo=gmo, sc_ap=sc_ap, g8=g8: nc.vector.scalar_tensor_tensor(
                out=gmo, in0=sc_ap, scalar=1.0, in1=g8, op0=ALU.add, op1=ALU.mult)),
                reads=['mods', 'V'], writes=['gm'])
    P.barrier()
    A.reset(mark0)

    def shift_ap(l, which, c, j):
        return mod_ap(l, which * 3 + 0, c, j)

    def gate_ap(l, which, c, j):
        return mod_ap(l, which * 3 + 2, c, j)

    sqb = [A.bf16(512) for _ in range(2)]
    stdb = A.f32(512)
    rstdb = [A.f32(512) for _ in range(2)]
    tmpb = [A.f32(512) for _ in range(2)]
    cnt = {'sq': 0, 'tmp': 0, 'rstd': 0}

    def norm_stage(xt, xtn, W, ps_i, out_h, out_hn, scale_fn, shift_fn, out_dtype_bf16=True):
        ps = psum[ps_i]
        for c in range(8):
            i = cnt['sq'] % 2
            cnt['sq'] += 1
            sb = sqb[i]
            P.op('act', (lambda sb=sb, c=c: nc.scalar.activation(out=sb[:, :W], in_=xt[:, c * W:(c + 1) * W],
                                                                  func=AF.Square)),
                 reads=['%s_%d' % (xtn, c)], writes=['sq%d' % i])
            P.op('pe', (lambda sb=sb, c=c: nc.tensor.matmul(out=ps[:, :W], lhsT=ones_bf, rhs=sb[:, :W],
                                                            start=(c == 0), stop=(c == 7))),
                 reads=['sq%d' % i, 'ones'], writes=[PS[ps_i]])
        P.op('act', lambda: nc.scalar.activation(out=stdb[:, :W], in_=ps[:, :W], func=AF.Ln,
                                                 scale=1.0 / D, bias=eps_ap),
             reads=[PS[ps_i], 'ones'], writes=['std'])
        ri = cnt['rstd'] % 2
        cnt['rstd'] += 1
        rs = rstdb[ri]
        P.op('act', lambda: nc.scalar.activation(out=rs[:, :W], in_=stdb[:, :W], func=AF.Exp, scale=-0.5),
             reads=['std'], writes=['rstd%d' % ri])
        for c in range(8):
            i = cnt['tmp'] % 2
            cnt['tmp'] += 1
            tb = tmpb[i]
            P.op('dve', (lambda tb=tb, c=c: nc.vector.scalar_tensor_tensor(
                out=tb[:, :W], in0=xt[:, c * W:(c + 1) * W], scalar=scale_fn(c), in1=rs[:, :W],
                op0=ALU.mult, op1=ALU.mult)),
                reads=['%s_%d' % (xtn, c), 'rstd%d' % ri, 'gm', 'V'], writes=['tmp%d' % i])
            sh = shift_fn(c)
            if sh is None:
                P.op('act', (lambda tb=tb, c=c: nc.scalar.activation(
                    out=out_h[:, c * W:(c + 1) * W], in_=tb[:, :W], func=AF.Copy)),
                    reads=['tmp%d' % i], writes=['%s_%d' % (out_hn, c)])
            else:
                P.op('act', (lambda tb=tb, c=c, sh=sh: nc.scalar.activation(
                    out=out_h[:, c * W:(c + 1) * W], in_=tb[:, :W], func=AF.Identity, bias=sh, scale=1.0)),
                    reads=['tmp%d' % i, 'mods'], writes=['%s_%d' % (out_hn, c)])

    def blks(name, c0, W):
        return ['%s_%d' % (name, i) for i in range(c0 // 256, (c0 + W - 1) // 256 + 1)]

    cur = {'ap': xin, 'name': 'xin'}

    def mlp_sweep(l, last, src_ap, src_name):
        m0 = A.mark()
        TW = 256
        w1 = A.bf16(8 * DFF)
        w2 = A.bf16(32 * D)
        w1v = w1.rearrange("p (k n) -> p k n", k=8)
        w2v = w2.rearrange("p (k n) -> p k n", k=32)
        for k in range(8 if cfg.get('wload', True) else 0):
            src = w_mlp1[l, k * 128:(k + 1) * 128, :]
            P.op('pool', (lambda k=k, src=src: nc.gpsimd.dma_start(out=w1v[:, k, :], in_=src, max_dma_last_dim=8192)),
                 writes=['w1_%d' % k], lane='d_w1_%d' % (k % 4))
        for k in range(32 if cfg.get('wload', True) else 0):
            src = w_mlp2[l, k * 128:(k + 1) * 128, :]
            P.op('pool', (lambda k=k, src=src: nc.gpsimd.dma_start(out=w2v[:, k, :], in_=src, max_dma_last_dim=8192)),
                 writes=['w2_%d' % k], lane='d_w2_%d' % (k % 4))
        xt = [A.f32(8 * TW) for _ in range(2)]
        hb = [A.bf16(8 * TW) for _ in range(2)]
        h1 = A.bf16(32 * TW)
        rl = [A.f32(TW) for _ in range(2)]
        fo = ([A.f32(8 * TW)] * 2) if last else None
        tiles = []
        if not last:
            tiles.append((0, 256, 1))
        for i in range(cfg.get('ntiles', SEQ // TW)):
            tiles.append((CTX + i * TW, TW, 0))
        n = len(tiles)
        st = {'w1ps': 0, 'w2ps': 0, 'rl': 0}

        def stage_load(ti):
            c0, W, j = tiles[ti]
            b = ti % 2
            x3 = xt[b].rearrange("p (c w) -> p c w", c=8)
            src = src_ap[:, c0:c0 + W].rearrange("(c p) w -> p c w", p=128)
            P.op('sp', (lambda x3=x3, src=src: nc.sync.dma_start(out=x3, in_=src)),
                 reads=blks(src_name, c0, W), writes=['xt%d_%d' % (b, c) for c in range(8)], lane='d_xt%d' % b)

        def stage_norm(ti):
            c0, W, j = tiles[ti]
            b = ti % 2
            norm_stage(xt[b], 'xt%d' % b, W, 0, hb[b], 'h%d' % b,
                       lambda c: gm_ap(l, 1, c, j), lambda c: shift_ap(l, 1, c, j))

        def stage_b(ti):
            c0, W, j = tiles[ti]
            b = ti % 2
            h3 = hb[b].rearrange("p (c w) -> p c w", c=8)
            h13 = h1.rearrange("p (c w) -> p c w", c=32)
            x3 = xt[b].rearrange("p (c w) -> p c w", c=8)
            if ti + 1 < n:
                stage_load(ti + 1)
            for m in range(32):
                pi = 1 + st['w1ps'] % 4
                st['w1ps'] += 1
                ps = psum[pi]
                for k in range(8):
                    P.op('pe', (lambda ps=ps, k=k, m=m: nc.tensor.matmul(
                        out=ps[:, :W], lhsT=w1v[:, k, m * 128:(m + 1) * 128], rhs=h3[:, k, :],
                        start=(k == 0), stop=(k == 7))),
                        reads=['w1_%d' % k, 'h%d_%d' % (b, k)], writes=[PS[pi]])
                ri = st['rl'] % 2
                st['rl'] += 1
                rb = rl[ri]
                P.op('act', (lambda ps=ps, rb=rb: nc.scalar.activation(out=rb[:, :W], in_=ps[:, :W], func=AF.Relu)),
                     reads=[PS[pi]], writes=['rl%d' % ri])
                P.op('dve', (lambda ps=ps, rb=rb, m=m: nc.vector.tensor_tensor(
                    out=h13[:, m, :], in0=ps[:, :W], in1=rb[:, :W], op=ALU.mult)),
                    reads=[PS[pi], 'rl%d' % ri], writes=['h1_%d' % m])
            if ti + 1 < n:
                stage_norm(ti + 1)
            for jc in range(8):
                pi = 5 + st['w2ps'] % 3
                st['w2ps'] += 1
                ps = psum[pi]
                for k in range(32):
                    P.op('pe', (lambda ps=ps, k=k, jc=jc: nc.tensor.matmul(
                        out=ps[:, :W], lhsT=w2v[:, k, jc * 128:(jc + 1) * 128], rhs=h13[:, k, :],
                        start=(k == 0), stop=(k == 31))),
                        reads=['w2_%d' % k, 'h1_%d' % k], writes=[PS[pi]])
                g2 = gate_ap(l, 1, jc, j)
                P.op('dve', (lambda ps=ps, jc=jc, g2=g2: nc.vector.scalar_tensor_tensor(
                    out=x3[:, jc, :], in0=ps[:, :W], scalar=g2, in1=x3[:, jc, :], op0=ALU.mult, op1=ALU.add)),
                    reads=[PS[pi], 'mods', 'xt%d_%d' % (b, jc)], writes=['xt%d_%d' % (b, jc)])
            if not last:
                dst = xs[:, c0:c0 + W].rearrange("(c p) w -> p c w", p=128)
                P.op('sp', (lambda x3=x3, dst=dst: nc.sync.dma_start(out=dst, in_=x3)),
                     reads=['xt%d_%d' % (b, c) for c in range(8)], writes=blks('xs', c0, W), lane='d_xo%d' % b)
            else:
                fb = fo[b]
                fo_off, _ = VEC_OFF['fin']
                norm_stage(xt[b], 'xt%d' % b, W, 0, fb, 'fo',
                           lambda c: V[:, fo_off + c:fo_off + c + 1], lambda c: None)
                f3 = fb.rearrange("p (c w) -> p c w", c=8)
                dst = out_d[:, c0 - CTX:c0 - CTX + W].rearrange("(c p) w -> p c w", p=128)
                P.op('sp', (lambda f3=f3, dst=dst: nc.sync.dma_start(out=dst, in_=f3)),
                     reads=['fo_%d' % c for c in range(8)], writes=['out_%d' % ti], lane='d_fo')
                outbufs.append('out_%d' % ti)

        stage_load(0)
        stage_norm(0)
        for ti in range(n):
            stage_b(ti)
        P.barrier()
        A.reset(m0)

    wl = {'n': 0}

    def wload(dst3, src2d, nk, name, col0, ncols):
        for k in range(nk):
            src = src2d[k * 128:(k + 1) * 128, col0:col0 + ncols]
            P.op('pool', (lambda k=k, src=src: nc.gpsimd.dma_start(out=dst3[:, k, 0:ncols], in_=src,
                                                                  max_dma_last_dim=8192)),
                 writes=['%s_%d' % (name, k)], lane='d_wl%d' % (wl['n'] % 4))
            wl['n'] += 1

    def seq_tiles(W, with_ctx=True):
        tl = [(i * W, W, 1) for i in range(CTX // W)] if with_ctx else []
        if W > CTX and with_ctx:
            tl = [(0, CTX, 1)]
        tl += [(CTX + i * W, W, 0) for i in range(SEQ // W)]
        return tl

    def load_x(xt_ap, xt_name, c0, W, src_ap, src_name, lane):
        x3 = xt_ap[:, :8 * W].rearrange("p (c w) -> p c w", c=8)
        src = src_ap[:, c0:c0 + W].rearrange("(c p) w -> p c w", p=128)
        P.op('sp', (lambda: nc.sync.dma_start(out=x3, in_=src)),
             reads=blks(src_name, c0, W), writes=['%s_%d' % (xt_name, c) for c in range(8)], lane=lane)
        return x3

    def store_x(x3, xt_name, c0, W, lane):
        dst = xs[:, c0:c0 + W].rearrange("(c p) w -> p c w", p=128)
        P.op('sp', (lambda: nc.sync.dma_start(out=dst, in_=x3)),
             reads=['%s_%d' % (xt_name, c) for c in range(8)], writes=blks('xs', c0, W), lane=lane)

    opc = {'n': 0}

    def outproj(y3, yname, wo3, woname, x3, xname, l, j, W, banks):
        for jc in range(8):
            pi = banks[opc['n'] % len(banks)]
            opc['n'] += 1
            ps = psum[pi]
            for k in range(8):
                P.op('pe', (lambda ps=ps, k=k, jc=jc: nc.tensor.matmul(
                    out=ps[:, :W], lhsT=wo3[:, k, jc * 128:(jc + 1) * 128], rhs=y3[:, k, :],
                    start=(k == 0), stop=(k == 7))),
                    reads=['%s_%d' % (woname, k), '%s_%d' % (yname, k)], writes=[PS[pi]])
            g1 = gate_ap(l, 0, jc, j)
            P.op('dve', (lambda ps=ps, jc=jc, g1=g1: nc.vector.scalar_tensor_tensor(
                out=x3[:, jc, :], in0=ps[:, :W], scalar=g1, in1=x3[:, jc, :], op0=ALU.mult, op1=ALU.add)),
                reads=[PS[pi], 'mods', '%s_%d' % (xname, jc)], writes=['%s_%d' % (xname, jc)])

    def lru_layer(l, src_ap, src_name):
        m0 = A.mark()
        W = 512
        win3 = A.bf16(8 * 2048).rearrange("p (k n) -> p k n", k=8)
        wload(win3, lru_w_in[0], 8, 'lwin', 0, 2048)
        xt = [A.f32(8 * W) for _ in range(2)]
        hb = [A.bf16(8 * W) for _ in range(2)]
        stg = [A.f32(4 * W) for _ in range(2)]
        tiles = seq_tiles(W)
        n = len(tiles)
        st = {'ps': 0, 'g': 0}

        def p0_load(ti):
            c0, Wt, j = tiles[ti]
            b = ti % 2
            load_x(xt[b], 'xt%d' % b, c0, Wt, src_ap, src_name, 'd_xt%d' % b)

        def p0_norm(ti):
            c0, Wt, j = tiles[ti]
            b = ti % 2
            norm_stage(xt[b], 'xt%d' % b, Wt, 0, hb[b], 'h%d' % b,
                       lambda c: gm_ap(l, 0, c, j), lambda c: shift_ap(l, 0, c, j))

        def p0_b(ti):
            c0, Wt, j = tiles[ti]
            b = ti % 2
            h3 = hb[b][:, :8 * Wt].rearrange("p (c w) -> p c w", c=8)
            if ti + 1 < n:
                p0_load(ti + 1)
            for og in range(4):
                if og == 2 and ti + 1 < n:
                    p0_norm(ti + 1)
                gi = st['g'] % 2
                st['g'] += 1
                sg3 = stg[gi][:, :4 * Wt].rearrange("p (c w) -> p c w", c=4)
                for oi in range(4):
                    oc = og * 4 + oi
                    pi = 1 + st['ps'] % 4
                    st['ps'] += 1
                    ps = psum[pi]
                    for k in range(8):
                        P.op('pe', (lambda ps=ps, k=k, oc=oc: nc.tensor.matmul(
                            out=ps[:, :Wt], lhsT=win3[:, k, oc * 128:(oc + 1) * 128], rhs=h3[:, k, :],
                            start=(k == 0), stop=(k == 7))),
                            reads=['lwin_%d' % k, 'h%d_%d' % (b, k)], writes=[PS[pi]])
                    if oi % 2 == 0:
                        P.op('act', (lambda ps=ps, oi=oi, sg3=sg3: nc.scalar.copy(out=sg3[:, oi, :], in_=ps[:, :Wt])),
                             reads=[PS[pi]], writes=['stg%d_%d' % (gi, oi)])
                    else:
                        P.op('dve', (lambda ps=ps, oi=oi, sg3=sg3: nc.vector.tensor_copy(out=sg3[:, oi, :], in_=ps[:, :Wt])),
                             reads=[PS[pi]], writes=['stg%d_%d' % (gi, oi)])
                dst = gx[og * 512:(og + 1) * 512, c0:c0 + Wt].rearrange("(c p) w -> p c w", p=128)
                P.op('sp', (lambda sg3=sg3, dst=dst: nc.sync.dma_start(out=dst, in_=sg3)),
                     reads=['stg%d_%d' % (gi, oi) for oi in range(4)],
                     writes=['gx%d_%s' % (og, bname) for bname in blks('b', c0, Wt)], lane='d_stg%d' % gi)

        p0_load(0)
        p0_norm(0)
        for ti in range(n):
            p0_b(ti)
        P.barrier()
        A.reset(m0)
        lru_dir(l, 0)
        lru_dir(l, 1, src_ap, src_name)

    def lru_dir(l, bwd, src_ap=None, src_name=None):
        m0 = A.mark()
        W = 256
        wa3 = A.bf16(8 * 128).rearrange("p (k n) -> p k n", k=8)
        wx3 = A.bf16(8 * 128).rearrange("p (k n) -> p k n", k=8)
        wload(wa3, lru_wab[('a', bwd)][0].rearrange("n d e -> (n d) e"), 8, 'lwa', 0, 128)
        wload(wx3, lru_wab[('x', bwd)][0].rearrange("n d e -> (n d) e"), 8, 'lwx', 0, 128)
        sfx = '_b' if bwd else '_f'
        nsp8 = A.f32(8)
        carry = A.f32(8)
        lam = V[:, VEC_OFF['lam' + sfx][0]:VEC_OFF['lam' + sfx][0] + 8]
        P.op('act', lambda: nc.scalar.activation(out=nsp8, in_=lam, func=AF.Exp, scale=-1.0), reads=['V'], writes=['nsp8'])
        P.op('act', lambda: nc.scalar.activation(out=nsp8, in_=nsp8, func=AF.Ln, bias=1.0, scale=1.0), reads=['nsp8'], writes=['nsp8'])
        P.op('dve', lambda: nc.vector.tensor_scalar(out=nsp8, in0=nsp8, scalar1=-8.0, scalar2=None, op0=ALU.mult),
             reads=['nsp8'], writes=['nsp8'])
        P.op('dve', lambda: nc.vector.memset(carry, 0.0), writes=['carry'])
        v3 = lambda a: a.rearrange("p (c w) -> p c w", c=8)
        xbh_ = [A.f32(8 * (W + 4)) for _ in range(2)]
        xc_ = [A.f32(8 * W) for _ in range(2)]
        xcb_ = [A.bf16(8 * W) for _ in range(2)]
        rb_ = [A.f32(8 * W) for _ in range(2)]
        ig_ = [A.f32(8 * W) for _ in range(2)]
        sb_ = [A.f32(8 * W) for _ in range(2)]
        hs = A.f32(8 * W)
        hs3 = v3(hs)
        if bwd:
            wo3 = A.bf16(8 * D).rearrange("p (k n) -> p k n", k=8)
            wload(wo3, lru_w_o[0], 8, 'lwo', 0, D)
            hf_ = [A.f32(8 * W)] * 2
            gt_ = [A.f32(8 * W) for _ in range(2)]
            xt_ = [A.f32(8 * W) for _ in range(2)]
            tb_ = [A.f32(8 * W) for _ in range(2)]
            yb = A.bf16(8 * W)
            y3 = v3(yb)
        tiles = seq_tiles(W)
        if bwd:
            tiles = [t for t in tiles if t[2] == 1][::-1] + [t for t in tiles if t[2] == 0][::-1]
        nt = len(tiles)
        st = {'ps': 0}
        bo = lambda nm, c: V[:, VEC_OFF[nm][0] + c:VEC_OFF[nm][0] + c + 1]

        def stage1(ti):
            c0, Wt, j = tiles[ti]
            b = ti % 2
            xbh3 = xbh_[b].rearrange("p (c w) -> p c w", c=8)
            xc, xcb, rb, ig = xc_[b], xcb_[b], rb_[b], ig_[b]
            xc3, xcb3, r3, ig3 = v3(xc), v3(xcb), v3(rb), v3(ig)
            XB = 'xbh%d' % b
            s0, s1 = (0, CTX) if j == 1 else (CTX, T)
            lo, hi = c0 - 2, c0 + Wt + 1
            a0 = 0
            if lo < s0:
                P.op('dve', lambda: nc.vector.memset(xbh3[:, :, 0:2], 0.0), writes=[XB])
                a0 = s0 - lo
                lo = s0
            if hi > s1:
                P.op('dve', lambda: nc.vector.memset(xbh3[:, :, Wt + 2:Wt + 3], 0.0), writes=[XB])
                hi = s1
            src = gx[D:2 * D, lo:hi].rearrange("(c p) w -> p c w", p=128)
            rd = []
            for og in (2, 3):
                rd += ['gx%d_%s' % (og, bname) for bname in blks('b', lo, hi - lo)]
            P.op('sp', (lambda: nc.sync.dma_start(out=xbh3[:, :, a0:a0 + hi - lo], in_=src)),
                 reads=rd, writes=[XB], lane='d_xbh%d' % b)
            if bwd:
                srcg = gx[0:D, c0:c0 + Wt].rearrange("(c p) w -> p c w", p=128)
                rdg = []
                for og in (0, 1):
                    rdg += ['gx%d_%s' % (og, bname) for bname in blks('b', c0, Wt)]
                P.op('sp', (lambda: nc.sync.dma_start(out=v3(gt_[b]), in_=srcg)), reads=rdg, writes=['gt%d' % b],
                     lane='d_gt%d' % b)
                load_x(xt_[b], 'lxt%d' % b, c0, Wt, src_ap, src_name, 'd_lxt%d' % b)
            for c in range(8):
                P.op('dve', (lambda c=c: nc.vector.tensor_scalar(
                    out=xc3[:, c, :], in0=xbh3[:, c, 0:Wt], scalar1=bo('lru_cw', 0 * 8 + c), scalar2=bo('lru_cb', c),
                    op0=ALU.mult, op1=ALU.add)), reads=[XB, 'V'], writes=['xc%d_%d' % (b, c)])
            for jt in range(1, 4):
                for c in range(8):
                    P.op('dve', (lambda c=c, jt=jt: nc.vector.scalar_tensor_tensor(
                        out=xc3[:, c, :], in0=xbh3[:, c, jt:jt + Wt], scalar=bo('lru_cw', jt * 8 + c),
                        in1=xc3[:, c, :], op0=ALU.mult, op1=ALU.add)), reads=[XB, 'V', 'xc%d_%d' % (b, c)],
                        writes=['xc%d_%d' % (b, c)])
            XC = ['xc%d_%d' % (b, c) for c in range(8)]
            P.op('act', lambda: nc.scalar.copy(out=xcb, in_=xc), reads=XC, writes=['xcb%d' % b])
            for (w3, wn, bn, dst3, dn) in ((wa3, 'lwa', 'ba' + sfx, r3, 'r%d' % b), (wx3, 'lwx', 'bx' + sfx, ig3, 'ig%d' % b)):
                for c in range(8):
                    pi = st['ps'] % 4
                    st['ps'] += 1
                    ps = psum[pi]
                    P.op('pe', (lambda ps=ps, c=c, w3=w3: nc.tensor.matmul(
                        out=ps[:, :Wt], lhsT=w3[:, c, :], rhs=xcb3[:, c, :], start=True, stop=True)),
                        reads=['%s_%d' % (wn, c), 'xcb%d' % b], writes=[PS[pi]])
                    P.op('act', (lambda ps=ps, c=c, dst3=dst3, bn=bn: nc.scalar.activation(
                        out=dst3[:, c, :], in_=ps[:, :Wt], func=AF.Sigmoid, bias=bo(bn, c), scale=1.0)),
                        reads=[PS[pi], 'V'], writes=['%s_%d' % (dn, c)])
            for c in range(8):
                P.op('act', (lambda c=c: nc.scalar.activation(out=r3[:, c, :], in_=r3[:, c, :], func=AF.Exp,
                                                              scale=nsp8[:, c:c + 1])),
                     reads=['r%d_%d' % (b, c), 'nsp8'], writes=['r%d_%d' % (b, c)])
            if bwd:
                gt, tb = gt_[b], tb_[b]
                GT, TB = 'gt%d' % b, 'tb%d' % b
                P.op('act', lambda: nc.scalar.activation(out=tb, in_=gt, func=AF.Square), reads=[GT], writes=[TB])
                P.op('act', lambda: nc.scalar.activation(out=tb, in_=tb, func=AF.Identity, scale=0.044715, bias=one_ap),
                     reads=[TB, 'ones'], writes=[TB])
                P.op('dve', lambda: nc.vector.tensor_tensor(out=tb, in0=tb, in1=gt, op=ALU.mult), reads=[TB, GT], writes=[TB])
                P.op('act', lambda: nc.scalar.activation(out=tb, in_=tb, func=AF.Sigmoid, scale=1.5957691216057308),
                     reads=[TB], writes=[TB])

        def stage1b(ti):
            b = ti % 2
            rb, sb = rb_[b], sb_[b]
            RA = ['r%d_%d' % (b, c) for c in range(8)]
            P.op('act', lambda: nc.scalar.activation(out=sb, in_=rb, func=AF.Square), reads=RA, writes=['s%d' % b])
            P.op('act', lambda: nc.scalar.activation(out=sb, in_=sb, func=AF.Sqrt, scale=-1.0, bias=1.0),
                 reads=['s%d' % b], writes=['s%d' % b])

        def stage2(ti):
            c0, Wt, j = tiles[ti]
            b = ti % 2
            xc, rb, ig, ub = xc_[b], rb_[b], ig_[b], sb_[b]
            r3, u3 = v3(rb), v3(ub)
            XC = ['xc%d_%d' % (b, c) for c in range(8)]
            IG = ['ig%d_%d' % (b, c) for c in range(8)]
            if bwd:
                srcf = of_d[:, c0:c0 + Wt].rearrange("(c p) w -> p c w", p=128)
                P.op('sp', (lambda: nc.sync.dma_start(out=v3(hf_[0]), in_=srcf)),
                     reads=blks('of', c0, Wt), writes=['hf'], lane='d_hf')
            P.op('dve', lambda: nc.vector.tensor_tensor(out=ig, in0=ig, in1=xc, op=ALU.mult), reads=IG + XC, writes=IG)
            P.op('dve', lambda: nc.vector.tensor_tensor(out=ub, in0=ub, in1=ig, op=ALU.mult), reads=['s%d' % b] + IG,
                 writes=['s%d' % b])
            for c in range(8):
                if bwd:
                    P.op('dve', (lambda c=c: nc.vector.tensor_tensor_scan(
                        out=hs3[:, c, ::-1], data0=r3[:, c, ::-1], data1=u3[:, c, ::-1], initial=carry[:, c:c + 1],
                        op0=ALU.mult, op1=ALU.add)), reads=['r%d_%d' % (b, c), 's%d' % b, 'carry'], writes=['hs_%d' % c])
                else:
                    P.op('dve', (lambda c=c: nc.vector.tensor_tensor_scan(
                        out=hs3[:, c, :], data0=r3[:, c, :], data1=u3[:, c, :], initial=carry[:, c:c + 1],
                        op0=ALU.mult, op1=ALU.add)), reads=['r%d_%d' % (b, c), 's%d' % b, 'carry'], writes=['hs_%d' % c])
            HS = ['hs_%d' % c for c in range(8)]
            last_col = 0 if bwd else Wt - 1
            P.op('dve', lambda: nc.vector.tensor_copy(out=carry, in_=hs3[:, :, last_col]), reads=HS, writes=['carry'])
            if not bwd:
                dst = of_d[:, c0:c0 + Wt].rearrange("(c p) w -> p c w", p=128)
                P.op('sp', (lambda: nc.sync.dma_start(out=dst, in_=hs3)), reads=HS, writes=blks('of', c0, Wt), lane='d_of')
                return
            hf, gt = hf_[b], gt_[b]
            x3 = v3(xt_[b])
            HF, GT = 'hf', 'gt%d' % b
            tb = tb_[b]
            TB = 'tb%d' % b
            P.op('dve', lambda: nc.vector.tensor_tensor(out=hf, in0=hf, in1=hs, op=ALU.add), reads=HS + [HF], writes=[HF])
            P.op('dve', lambda: nc.vector.tensor_tensor(out=hf, in0=hf, in1=gt, op=ALU.mult), reads=[HF, GT], writes=[HF])
            P.op('dve', lambda: nc.vector.tensor_tensor(out=yb, in0=hf, in1=tb, op=ALU.mult), reads=[HF, TB],
                 writes=['ly_%d' % c for c in range(8)])
            outproj(y3, 'ly', wo3, 'lwo', x3, 'lxt%d' % b, l, j, Wt, [4, 5, 6, 7])
            store_x(x3, 'lxt%d' % b, c0, Wt, 'd_lxo')

        stage1(0)
        stage1b(0)
        for ti in range(nt):
            if ti + 1 < nt:
                stage1(ti + 1)
            stage2(ti)
            if ti + 1 < nt:
                stage1b(ti + 1)
        P.barrier()
        A.reset(m0)

    def gla_layer(l, jl, src_ap, src_name, need_ctx):
        m0 = A.mark()
        win3 = A.bf16(8 * 3104).rearrange("p (k n) -> p k n", k=8)
        wload(win3, gla_w_in[jl], 8, 'gwin', 0, 3104)
        wo3 = A.bf16(8 * D).rearrange("p (k n) -> p k n", k=8)
        m1 = A.mark()
        gla_dir(l, jl, 0, src_ap, src_name, need_ctx, win3, wo3)
        A.reset(m1)
        gla_dir(l, jl, 1, src_ap, src_name, need_ctx, win3, wo3)
        P.barrier()
        A.reset(m0)

    def gla_dir(l, jl, bwd, src_ap, src_name, need_ctx, win3, wo3):
        W = 256
        NCK = W // GCK
        wup = A.f32(512)
        wup_src = (gla_w_up_b if bwd else gla_w_up_f)[jl]
        P.op('sp', lambda: nc.sync.dma_start(out=wup[0:16, :], in_=wup_src), writes=['gwup'], lane='d_gwup')
        negb = A.f32(4)
        bname = 'gla_bb' if bwd else 'gla_bf'
        bsrc = V[:, VEC_OFF[bname][0] + jl * 4:VEC_OFF[bname][0] + jl * 4 + 4]
        P.op('dve', lambda: nc.vector.tensor_scalar(out=negb, in0=bsrc, scalar1=-1.0, scalar2=None, op0=ALU.mult),
             reads=['V'], writes=['negb'])
        if not bwd:
            wload(wo3, gla_w_o[jl], 8, 'gwo', 0, D)
        xts = [A.f32(8 * W) for _ in range(2)]
        hbs = [A.bf16(8 * W) for _ in range(2)]
        q32, k32, spb, Bb, Eb = (A.f32(4 * W) for _ in range(5))
        gfs = A.f32(W)
        qt, kt, kpT = (A.bf16(4 * W) for _ in range(3))
        kptok = A.bf16(2 * 512)
        vtok = A.bf16(2 * 1024)
        Ab = [A.bf16(512) for _ in range(2)]
        S32 = A.f32(4 * 256)
        Sbf = A.bf16(4 * 256)
        E3 = A.f32(4 * NCK)
        ot = A.f32(8 * W)
        v8 = lambda a: a.rearrange("p (c w) -> p c w", c=8)
        v4 = lambda a: a.rearrange("p (c w) -> p c w", c=4)
        ot3 = v8(ot)
        q3, k3, sp3, B3, E3v = v4(q32), v4(k32), v4(spb), v4(Bb), v4(Eb)
        qt3, kt3, kp3 = v4(qt), v4(kt), v4(kpT)
        B4 = Bb.rearrange("p (h c t) -> p h c t", h=4, t=GCK)
        E4 = Eb.rearrange("p (h c t) -> p h c t", h=4, t=GCK)
        kptok3 = kptok.rearrange("p (b n) -> p b n", b=2)
        vtok3 = vtok.rearrange("p (b n) -> p b n", b=2)
        S3 = v4(S32)
        Sb3 = v4(Sbf)
        E33 = E3.rearrange("p (h c) -> p h c", h=4)
        ofls = [A.f32(8 * W) for _ in range(2)]
        rsb = A.f32(8 * W)
        yb = A.bf16(8 * W)
        sq8 = A.bf16(8 * W)
        rs3, y3, sq83 = v8(rsb), v8(yb), v8(sq8)
        P.op('dve', lambda: nc.vector.memset(S32, 0.0), writes=['S32_%d' % h for h in range(4)])
        P.op('dve', lambda: nc.vector.memset(Sbf, 0.0), writes=['Sbf_%d' % h for h in range(4)])
        tiles = seq_tiles(W)
        if bwd:
            tiles = [t for t in tiles if t[2] == 1][::-1] + [t for t in tiles if t[2] == 0][::-1]
        st = {'pp': 0, 'ab': 0}
        maskc = C_MASKB if bwd else C_MASKF
        tric = C_TRIB if bwd else C_TRIF
        tri = CST[:, tric:tric + 128].unsqueeze(1).broadcast_to([128, 4, 128])
        gcol = 3088 if bwd else 3072
        ngo = VEC_OFF['gla_ng'][0] + jl * 2

        def pbank():
            pi = 1 + st['pp'] % 2
            st['pp'] += 1
            return pi

        nt = len(tiles)

        def stage_a(ti):
            c0, Wt, j = tiles[ti]
            bb = ti % 2
            need_out = (j == 0) or need_ctx
            load_x(xts[bb], 'gxt%d' % bb, c0, Wt, src_ap, src_name, 'd_gxt%d' % bb)
            if bwd and need_out:
                srcf = of_d[:, c0:c0 + Wt].rearrange("(c p) w -> p c w", p=128)
                ofl3 = v8(ofls[bb])
                P.op('sp', (lambda srcf=srcf, ofl3=ofl3: nc.sync.dma_start(out=ofl3, in_=srcf)),
                     reads=blks('of', c0, Wt), writes=['ofl%d' % bb], lane='d_ofl%d' % bb)
            norm_stage(xts[bb], 'gxt%d' % bb, Wt, 0, hbs[bb], 'gh%d' % bb,
                       (lambda c, j=j: gm_ap(l, 0, c, j)), (lambda c, j=j: shift_ap(l, 0, c, j)))

        pre = set()

        def proj_qk(ti, on_dve):
            c0, Wt, j = tiles[ti]
            bb = ti % 2
            h3 = v8(hbs[bb])
            GH = 'gh%d' % bb
            for (col, dst3, dn, scl) in ((0, q3, 'q32', 128.0 ** -0.5), (512, k3, 'k32', 1.0)):
                for h in range(4):
                    pi = pbank()
                    ps = psum[pi]
                    for k in range(8):
                        P.op('pe', (lambda ps=ps, k=k, h=h, col=col: nc.tensor.matmul(
                            out=ps[:, :Wt], lhsT=win3[:, k, col + h * 128:col + (h + 1) * 128], rhs=h3[:, k, :],
                            start=(k == 0), stop=(k == 7))), reads=['gwin_%d' % k, '%s_%d' % (GH, k)], writes=[PS[pi]])
                    if on_dve:
                        P.op('dve', (lambda ps=ps, h=h, dst3=dst3, scl=scl: nc.vector.tensor_scalar(
                            out=dst3[:, h, :], in0=ps[:, :Wt], scalar1=scl, scalar2=None, op0=ALU.mult)),
                            reads=[PS[pi]], writes=['%s_%d' % (dn, h)])
                    else:
                        P.op('act', (lambda ps=ps, h=h, dst3=dst3, scl=scl: nc.scalar.mul(out=dst3[:, h, :], in_=ps[:, :Wt], mul=scl)),
                             reads=[PS[pi]], writes=['%s_%d' % (dn, h)])
            pi = pbank()
            ps = psum[pi]
            for k in range(8):
                P.op('pe', (lambda ps=ps, k=k: nc.tensor.matmul(
                    out=ps[0:16, :Wt], lhsT=win3[:, k, gcol:gcol + 16], rhs=h3[:, k, :], start=(k == 0), stop=(k == 7))),
                    reads=['gwin_%d' % k, '%s_%d' % (GH, k)], writes=[PS[pi]])
            P.op('dve', (lambda ps=ps: nc.vector.tensor_copy(out=gfs[0:16, :Wt], in_=ps[0:16, :Wt])), reads=[PS[pi]], writes=['gfs'])

        def stage_b(ti):
            c0, Wt, j = tiles[ti]
            bb = ti % 2
            need_out = (j == 0) or need_ctx
            x3 = v8(xts[bb])
            h3 = v8(hbs[bb])
            GH = 'gh%d' % bb
            GX = 'gxt%d' % bb
            if ti not in pre:
                proj_qk(ti, False)
            for h in range(4):
                pi = pbank()
                ps = psum[pi]
                P.op('pe', (lambda ps=ps, h=h: nc.tensor.matmul(
                    out=ps[:, :Wt], lhsT=wup[0:16, h * 128:(h + 1) * 128], rhs=gfs[0:16, :Wt], start=True, stop=True)),
                    reads=['gwup', 'gfs'], writes=[PS[pi]])
                P.op('act', (lambda ps=ps, h=h: nc.scalar.activation(out=E3v[:, h, :], in_=ps[:, :Wt], func=AF.Exp,
                                                                      scale=-1.0, bias=negb[:, h:h + 1])),
                     reads=[PS[pi], 'negb'], writes=['E'])
            P.op('act', lambda: nc.scalar.activation(out=spb, in_=Eb, func=AF.Ln, bias=1.0, scale=1.0), reads=['E'], writes=['sp'])
            for h in range(4):
                if bwd:
                    P.op('dve', (lambda h=h: nc.vector.tensor_tensor_scan(
                        out=B3[:, h, ::-1], data0=CST[:, maskc:maskc + Wt][:, ::-1], data1=sp3[:, h, ::-1], initial=0.0,
                        op0=ALU.mult, op1=ALU.add)), reads=['sp', 'CST'], writes=['B'])
                else:
                    P.op('dve', (lambda h=h: nc.vector.tensor_tensor_scan(
                        out=B3[:, h, :], data0=CST[:, maskc:maskc + Wt], data1=sp3[:, h, :], initial=0.0,
                        op0=ALU.mult, op1=ALU.add)), reads=['sp', 'CST'], writes=['B'])
            eidx = 0 if bwd else GCK - 1
            bend = B4[:, :, :, eidx]
            P.op('act', lambda: nc.scalar.activation(out=Eb, in_=Bb, func=AF.Exp, scale=-1.0 / 16), reads=['B'], writes=['E'])
            P.op('dve', lambda: nc.vector.tensor_tensor(out=qt, in0=q32, in1=Eb, op=ALU.mult),
                 reads=['E'] + ['q32_%d' % h for h in range(4)], writes=['qt'])
            P.op('act', lambda: nc.scalar.activation(out=Eb, in_=Bb, func=AF.Exp, scale=1.0 / 16), reads=['B', 'qt'], writes=['E'])
            P.op('dve', lambda: nc.vector.tensor_tensor(out=kt, in0=k32, in1=Eb, op=ALU.mult),
                 reads=['E'] + ['k32_%d' % h for h in range(4)], writes=['kt'])
            P.op('act', lambda: nc.scalar.activation(out=E33, in_=bend, func=AF.Exp, scale=-1.0 / 16), reads=['B'], writes=['E3'])
            P.op('dve', lambda: nc.vector.tensor_tensor(out=E4, in0=B4, in1=bend.unsqueeze(3).broadcast_to([128, 4, NCK, GCK]),
                                                        op=ALU.subtract), reads=['B', 'kt'], writes=['E'])
            P.op('act', lambda: nc.scalar.activation(out=Eb, in_=Eb, func=AF.Exp, scale=1.0 / 16), reads=['E'], writes=['E'])
            P.op('dve', lambda: nc.vector.tensor_tensor(out=kpT, in0=k32, in1=Eb, op=ALU.mult),
                 reads=['E'] + ['k32_%d' % h for h in range(4)], writes=['kpT'])
            vbanks = (3, 5, 6, 7)
            for blk in range(2):
                for half in range(2):
                    pi = vbanks[blk * 2 + half]
                    ps = psum[pi]
                    for k in range(8):
                        P.op('pe', (lambda ps=ps, k=k, blk=blk, half=half: nc.tensor.matmul(
                            out=ps[:, :512], lhsT=h3[:, k, blk * 128:(blk + 1) * 128],
                            rhs=win3[:, k, 1024 + half * 512:1024 + (half + 1) * 512], start=(k == 0), stop=(k == 7))),
                            reads=['gwin_%d' % k, '%s_%d' % (GH, k)], writes=[PS[pi]])
            if bwd and need_out:
                rbanks = (1, 2, 4, 0)
                for rg in range(4):
                    pi = rbanks[rg]
                    ps = psum[pi]
                    for r2 in range(2):
                        rc = rg * 2 + r2
                        for k in range(8):
                            P.op('pe', (lambda ps=ps, k=k, rc=rc, r2=r2: nc.tensor.matmul(
                                out=ps[:, r2 * 256:r2 * 256 + Wt], lhsT=win3[:, k, 2048 + rc * 128:2048 + (rc + 1) * 128],
                                rhs=h3[:, k, :], start=(k == 0), stop=(k == 7))),
                                reads=['gwin_%d' % k, '%s_%d' % (GH, k)], writes=[PS[pi]])
            for blk in range(2):
                for half in range(2):
                    pi = vbanks[blk * 2 + half]
                    ps = psum[pi]
                    if half == 0:
                        P.op('dve', (lambda ps=ps, blk=blk, half=half: nc.vector.tensor_copy(
                            out=vtok3[:, blk, half * 512:(half + 1) * 512], in_=ps[:, :512])),
                            reads=[PS[pi]], writes=['vtok_%d_%d' % (blk, half)])
                    else:
                        P.op('act', (lambda ps=ps, blk=blk, half=half: nc.scalar.copy(
                            out=vtok3[:, blk, half * 512:(half + 1) * 512], in_=ps[:, :512])),
                            reads=[PS[pi]], writes=['vtok_%d_%d' % (blk, half)])
            if bwd and need_out:
                for rg in range(4):
                    pi = rbanks[rg]
                    P.op('act', (lambda pi=pi, rg=rg: nc.scalar.activation(
                        out=rs3[:, rg * 2:rg * 2 + 2, :], in_=psum[pi][:, 0:512].rearrange("p (c w) -> p c w", c=2),
                        func=AF.Silu)), reads=[PS[pi]], writes=['rs_%d' % (rg * 2), 'rs_%d' % (rg * 2 + 1)])
            for blk in range(2):
                pti = pbank()
                psT = psum[pti][:, :].bitcast(BF16)
                for h in range(4):
                    P.op('pe', (lambda h=h, blk=blk, psT=psT: nc.tensor.transpose(
                        out=psT[:, h * 128:(h + 1) * 128], in_=kp3[:, h, blk * 128:(blk + 1) * 128], identity=ident_bf)),
                        reads=['kpT', 'ident'], writes=[PS[pti]])
                P.op('act', (lambda blk=blk, psT=psT: nc.scalar.copy(out=kptok3[:, blk, :], in_=psT[:, 0:512])),
                     reads=[PS[pti]], writes=['kptok_%d' % blk])
            if ti + 1 < nt:
                stage_a(ti + 1)
            blk_order = [1, 0] if bwd else [0, 1]
            if need_out:
                for blk in blk_order:
                    ai = blk
                    Aa = Ab[ai]
                    sci = pbank()
                    for h in range(4):
                        P.op('pe', (lambda blk=blk, h=h, sci=sci: nc.tensor.matmul(
                            out=psum[sci][:, h * 128:(h + 1) * 128],
                            lhsT=kt3[:, h, blk * 128:(blk + 1) * 128], rhs=qt3[:, h, blk * 128:(blk + 1) * 128],
                            start=True, stop=True)), reads=['kt', 'qt'], writes=[PS[sci]])
                    P.op('dve', (lambda Aa=Aa, sci=sci: nc.vector.tensor_tensor(
                        out=Aa.rearrange("p (h t) -> p h t", h=4), in0=psum[sci][:, 0:512].rearrange("p (h t) -> p h t", h=4),
                        in1=tri, op=ALU.mult)), reads=[PS[sci], 'CST'], writes=['A%d' % ai])
            pso = (4, 0)
            for blk in blk_order:
                ai = blk
                Aa = Ab[ai]
                ci = blk
                for h in range(4):
                    pi = (3, 5, 6, 7)[h]
                    P.op('pe', (lambda h=h, pi=pi, blk=blk: nc.tensor.matmul(
                        out=psum[pi][:, 0:256], lhsT=kptok3[:, blk, h * 128:(h + 1) * 128],
                        rhs=vtok3[:, blk, h * 256:(h + 1) * 256], start=True, stop=True)),
                        reads=['kptok_%d' % blk, 'vtok_%d_%d' % (blk, h // 2)], writes=[PS[pi]])
                if need_out:
                    for h in range(4):
                        for vc in range(2):
                            oc = h * 2 + vc
                            po_i = pso[oc // 4]
                            osl = slice((oc % 4) * 128, (oc % 4 + 1) * 128)
                            P.op('pe', (lambda h=h, vc=vc, po_i=po_i, osl=osl, blk=blk, Aa=Aa: nc.tensor.matmul(
                                out=psum[po_i][:, osl],
                                lhsT=vtok3[:, blk, h * 256 + vc * 128:h * 256 + (vc + 1) * 128],
                                rhs=Aa[:, h * 128:(h + 1) * 128], start=True, stop=False)),
                                reads=['vtok_%d_%d' % (blk, h // 2), 'A%d' % ai], writes=[PS[po_i]])
                            P.op('pe', (lambda h=h, vc=vc, po_i=po_i, osl=osl, blk=blk: nc.tensor.matmul(
                                out=psum[po_i][:, osl], lhsT=Sb3[:, h, vc * 128:(vc + 1) * 128],
                                rhs=qt3[:, h, blk * 128:(blk + 1) * 128], start=False, stop=True)),
                                reads=['Sbf_%d' % h, 'qt'], writes=[PS[po_i]])
                    for g2 in range(2):
                        po_i = pso[g2]
                        P.op('act', (lambda g2=g2, po_i=po_i, blk=blk: nc.scalar.copy(
                            out=ot3[:, g2 * 4:(g2 + 1) * 4, blk * 128:(blk + 1) * 128],
                            in_=psum[po_i][:, 0:512].rearrange("p (c t) -> p c t", c=4))),
                            reads=[PS[po_i]], writes=['ot'])
                for h in range(4):
                    pi = (3, 5, 6, 7)[h]
                    P.op('dve', (lambda h=h, pi=pi, ci=ci: nc.vector.scalar_tensor_tensor(
                        out=S3[:, h, :], in0=S3[:, h, :], scalar=E3[:, h * NCK + ci:h * NCK + ci + 1],
                        in1=psum[pi][:, 0:256], op0=ALU.mult, op1=ALU.add)),
                        reads=[PS[pi], 'E3', 'S32_%d' % h], writes=['S32_%d' % h])
                    P.op('act', (lambda h=h: nc.scalar.copy(out=Sb3[:, h, :], in_=S3[:, h, :])),
                         reads=['S32_%d' % h], writes=['Sbf_%d' % h])
            if not need_out:
                return
            if not bwd:
                dst = of_d[:, c0:c0 + Wt].rearrange("(c p) w -> p c w", p=128)
                P.op('sp', (lambda dst=dst: nc.sync.dma_start(out=dst, in_=ot3)), reads=['ot'], writes=blks('of', c0, Wt),
                     lane='d_of')
                return
            ofl = ofls[bb]
            P.op('dve', lambda: nc.vector.tensor_tensor(out=ot, in0=ot, in1=ofl, op=ALU.add), reads=['ot', 'ofl%d' % bb], writes=['ot'])
            P.op('act', lambda: nc.scalar.activation(out=sq8, in_=ot, func=AF.Square), reads=['ot'], writes=['sq8'])
            if ti + 1 < nt:
                proj_qk(ti + 1, True)
                pre.add(ti + 1)
            nb_ = [pbank(), pbank()]
            for h in range(4):
                pi = nb_[h // 2]
                for vc in range(2):
                    oc = h * 2 + vc
                    P.op('pe', (lambda pi=pi, h=h, vc=vc, oc=oc: nc.tensor.matmul(
                        out=psum[pi][:, (h % 2) * 256:(h % 2) * 256 + Wt], lhsT=ones_bf, rhs=sq83[:, oc, :],
                        start=(vc == 0), stop=(vc == 1))), reads=['sq8', 'ones'], writes=[PS[pi]])
            for hh in range(2):
                pi = nb_[hh]
                P.op('act', (lambda pi=pi, hh=hh: nc.scalar.activation(out=Eb[:, hh * 512:(hh + 1) * 512], in_=psum[pi][:, 0:512],
                                                                        func=AF.Ln, scale=1.0 / 256, bias=eps_ap)),
                     reads=[PS[pi], 'ones'], writes=['E'])
            P.op('act', lambda: nc.scalar.activation(out=Bb, in_=Eb, func=AF.Exp, scale=-0.5), reads=['E'], writes=['B'])
            for h in range(4):
                for vc in range(2):
                    oc = h * 2 + vc
                    P.op('dve', (lambda oc=oc, vc=vc, h=h: nc.vector.scalar_tensor_tensor(
                        out=ot3[:, oc, :], in0=ot3[:, oc, :], scalar=V[:, ngo + vc:ngo + vc + 1], in1=B3[:, h, :],
                        op0=ALU.mult, op1=ALU.mult)), reads=['ot', 'B', 'V'], writes=['ot'])
            P.op('dve', lambda: nc.vector.tensor_tensor(out=yb, in0=ot, in1=rsb, op=ALU.mult),
                 reads=['ot'] + ['rs_%d' % rc for rc in range(8)], writes=['gy_%d' % c for c in range(8)])
            outproj(y3, 'gy', wo3, 'gwo', x3, GX, l, j, Wt, [1, 2])
            store_x(x3, GX, c0, Wt, 'd_gxo')

        stage_a(0)
        for ti in range(nt):
            stage_b(ti)

    def attn_layer(l, src_ap, src_name, need_ctx):
        m0 = A.mark()
        W = 512
        win3 = A.bf16(8 * 1536).rearrange("p (k n) -> p k n", k=8)
        wload(win3, attn_w_in[0], 8, 'awin', 0, 1536)
        wo3 = A.bf16(8 * D).rearrange("p (k n) -> p k n", k=8)
        wload(wo3, attn_w_o[0], 8, 'awo', 0, D)
        KT3 = A.bf16(2 * T).rearrange("p (g t) -> p g t", g=2)
        Vt3 = A.bf16(34 * 256).rearrange("p (b n) -> p b n", b=34)
        rot_f = CST[:, C_ROT:C_ROT + 128]
        cosb, sinb = A.f32(W), A.f32(W)
        xt = A.f32(8 * W)
        hb = A.bf16(8 * W)
        xt2 = [xt, A.f32(8 * W)]
        hb2 = [hb, A.bf16(8 * W)]
        knb2 = [A.f32(W) for _ in range(2)]
        t1b2 = [A.f32(W) for _ in range(2)]
        t2b2 = [A.f32(W) for _ in range(2)]
        lnb2 = [A.f32(W) for _ in range(2)]
        qh = A.bf16(8 * W)
        yb = A.bf16(8 * W)
        Pt = [A.bf16(W) for _ in range(3)]
        rden = A.f32(W)
        st = {'sb': 0, 'pt': 0, 'hd': 0}
        SC = 128.0 ** -0.5

        def sbank():
            pi = 2 + st['sb'] % 2
            st['sb'] += 1
            return pi

        def load_rope(c0, Wt):
            for tb, idx, nm in ((cosb, 0, 'cos'), (sinb, 1, 'sin')):
                src = rope_d[idx, :, c0 - CTX:c0 - CTX + Wt]
                P.op('sp', (lambda tb=tb, src=src: nc.sync.dma_start(out=tb[:, :Wt], in_=src)), writes=[nm], lane='d_' + nm)

        qn = {'n': 0, 'pj': 0}

        def qk_norm_rope(ps, psn, gname, dst, dstn, Wt, is_lat):
            i = cnt['sq'] % 2
            cnt['sq'] += 1
            sb = sqb[i]
            k2 = qn['n'] % 2
            qn['n'] += 1
            kn_, t1_, t2_, ln_ = knb2[k2], t1b2[k2], t2b2[k2], lnb2[k2]
            P.op('act', lambda: nc.scalar.activation(out=sb[:, :Wt], in_=ps[:, :Wt], func=AF.Square), reads=[psn], writes=['sq%d' % i])
            pq = sbank()
            P.op('pe', lambda: nc.tensor.matmul(out=psum[pq][:, :Wt], lhsT=ones_bf, rhs=sb[:, :Wt], start=True, stop=True),
                 reads=['sq%d' % i, 'ones'], writes=[PS[pq]])
            P.op('act', lambda: nc.scalar.activation(out=ln_[:, :Wt], in_=psum[pq][:, :Wt], func=AF.Ln, scale=1.0 / 128, bias=eps_ap),
                 reads=[PS[pq], 'ones'], writes=['ln%d' % k2])
            ri = cnt['rstd'] % 2
            cnt['rstd'] += 1
            rs_ = rstdb[ri]
            P.op('act', lambda: nc.scalar.activation(out=rs_[:, :Wt], in_=ln_[:, :Wt], func=AF.Exp, scale=-0.5),
                 reads=['ln%d' % k2], writes=['rstd%d' % ri])
            g = V[:, VEC_OFF[gname][0]:VEC_OFF[gname][0] + 1]
            if not is_lat:
                P.op('dve', lambda: nc.vector.scalar_tensor_tensor(out=dst, in0=ps[:, :Wt], scalar=g, in1=rs_[:, :Wt],
                                                                   op0=ALU.mult, op1=ALU.mult),
                     reads=[psn, 'rstd%d' % ri, 'V'], writes=[dstn])
                return
            P.op('dve', lambda: nc.vector.scalar_tensor_tensor(out=kn_[:, :Wt], in0=ps[:, :Wt], scalar=g, in1=rs_[:, :Wt],
                                                               op0=ALU.mult, op1=ALU.mult),
                 reads=[psn, 'rstd%d' % ri, 'V'], writes=['kn%d' % k2])
            pr = sbank()
            P.op('pe', lambda: nc.tensor.matmul(out=psum[pr][:, :Wt], lhsT=rot_f, rhs=kn_[:, :Wt], start=True, stop=True),
                 reads=['kn%d' % k2, 'CST'], writes=[PS[pr]])
            P.op('dve', lambda: nc.vector.tensor_tensor(out=t1_[:, :Wt], in0=kn_[:, :Wt], in1=cosb[:, :Wt], op=ALU.mult),
                 reads=['kn%d' % k2, 'cos'], writes=['t1%d' % k2])
            P.op('dve', lambda: nc.vector.tensor_tensor(out=t2_[:, :Wt], in0=psum[pr][:, :Wt], in1=sinb[:, :Wt], op=ALU.mult),
                 reads=[PS[pr], 'sin'], writes=['t2%d' % k2])
            P.op('dve', lambda: nc.vector.tensor_tensor(out=dst, in0=t1_[:, :Wt], in1=t2_[:, :Wt], op=ALU.add),
                 reads=['t1%d' % k2, 't2%d' % k2], writes=[dstn])

        def proj(h3, col, Wt, hname='ah'):
            pj = qn['pj'] % 2
            qn['pj'] += 1
            ps = psum[pj]
            for k in range(8):
                P.op('pe', (lambda k=k: nc.tensor.matmul(out=ps[:, :Wt], lhsT=win3[:, k, col:col + 128], rhs=h3[:, k, :],
                                                         start=(k == 0), stop=(k == 7))),
                     reads=['awin_%d' % k, '%s_%d' % (hname, k)], writes=[PS[pj]])
            return ps, PS[pj]

        def pass1(c0, Wt, j):
            load_x(xt, 'axt', c0, Wt, src_ap, src_name, 'd_axt')
            if j == 0:
                load_rope(c0, Wt)
            norm_stage(xt, 'axt', Wt, 0, hb, 'ah', lambda c: gm_ap(l, 0, c, j), lambda c: shift_ap(l, 0, c, j))
            h3 = hb[:, :8 * Wt].rearrange("p (c w) -> p c w", c=8)
            for g in range(2):
                ps, psn = proj(h3, 1024 + g * 128, Wt)
                qk_norm_rope(ps, psn, 'ak_g', KT3[:, g, c0:c0 + Wt], 'KT_%d_%d' % (g, c0), Wt, j == 0)
            for blk in range(Wt // 128):
                pi = sbank()
                ps = psum[pi]
                gb = c0 // 128 + blk
                for k in range(8):
                    P.op('pe', (lambda ps=ps, k=k, blk=blk: nc.tensor.matmul(
                        out=ps[:, :256], lhsT=h3[:, k, blk * 128:(blk + 1) * 128], rhs=win3[:, k, 1280:1536],
                        start=(k == 0), stop=(k == 7))), reads=['awin_%d' % k, 'ah_%d' % k], writes=[PS[pi]])
                P.op('act', (lambda ps=ps, gb=gb: nc.scalar.copy(out=Vt3[:, gb, :], in_=ps[:, :256])),
                     reads=[PS[pi]], writes=['Vt_%d' % gb])

        def p2_load(qi):
            c0, Wt, j = qtiles[qi]
            b = qi % 2
            load_x(xt2[b], ('axt', 'axq1')[b], c0, Wt, src_ap, src_name, ('d_axt', 'd_axq1')[b])

        def p2_norm(qi):
            c0, Wt, j = qtiles[qi]
            b = qi % 2
            norm_stage(xt2[b], ('axt', 'axq1')[b], Wt, 0, hb2[b], ('ah', 'ahq1')[b],
                       lambda c: gm_ap(l, 0, c, j), lambda c: shift_ap(l, 0, c, j))

        def pass2(qi):
            c0, Wt, j = qtiles[qi]
            b = qi % 2
            AX, AH = ('axt', 'axq1')[b], ('ah', 'ahq1')[b]
            x3 = xt2[b][:, :8 * Wt].rearrange("p (c w) -> p c w", c=8)
            if qi + 1 < len(qtiles):
                p2_load(qi + 1)
            if j == 0:
                load_rope(c0, Wt)
            h3 = hb2[b][:, :8 * Wt].rearrange("p (c w) -> p c w", c=8)
            q3 = qh[:, :8 * Wt].rearrange("p (c w) -> p c w", c=8)
            y3 = yb[:, :8 * Wt].rearrange("p (c w) -> p c w", c=8)
            for hd in range(8):
                ps, psn = proj(h3, hd * 128, Wt, AH)
                qk_norm_rope(ps, psn, 'aq_g', q3[:, hd, :], 'qh_%d' % hd, Wt, j == 0)
            ktiles = list(range(2)) if j == 1 else list(range(34))
            nk = len(ktiles)
            def do_head(hd):
                g = hd // 4
                po_i = 4 + st['hd'] % 2
                pd_i = 6 + st['hd'] % 2
                st['hd'] += 1
                po, pd = psum[po_i], psum[pd_i]

                def kread(kt):
                    c = kt * 128
                    if c < CTX:
                        return 'KT_%d_%d' % (g, 0)
                    return 'KT_%d_%d' % (g, CTX + ((c - CTX) // 512) * 512)

                def s_mm(kt):
                    pi = sbank()
                    P.op('pe', (lambda pi=pi, kt=kt: nc.tensor.matmul(
                        out=psum[pi][:, :Wt], lhsT=KT3[:, g, kt * 128:(kt + 1) * 128], rhs=q3[:, hd, :],
                        start=True, stop=True)), reads=[kread(kt), 'qh_%d' % hd], writes=[PS[pi]])
                    return pi

                cur_s = s_mm(ktiles[0])
                for ii, kt in enumerate(ktiles):
                    nxt_s = s_mm(ktiles[ii + 1]) if ii + 1 < nk else None
                    pti = st['pt'] % 3
                    st['pt'] += 1
                    pt = Pt[pti]
                    P.op('act', (lambda cur_s=cur_s, pt=pt: nc.scalar.activation(
                        out=pt[:, :Wt], in_=psum[cur_s][:, :Wt], func=AF.Exp, scale=SC)),
                        reads=[PS[cur_s]], writes=['Pt%d' % pti])
                    P.op('pe', (lambda kt=kt, pt=pt, ii=ii: nc.tensor.matmul(
                        out=po[:, :Wt], lhsT=Vt3[:, kt, g * 128:(g + 1) * 128], rhs=pt[:, :Wt],
                        start=(ii == 0), stop=(ii == nk - 1))), reads=['Vt_%d' % kt, 'Pt%d' % pti], writes=[PS[po_i]])
                    P.op('pe', (lambda pt=pt, ii=ii: nc.tensor.matmul(
                        out=pd[:, :Wt], lhsT=ones_bf, rhs=pt[:, :Wt], start=(ii == 0), stop=(ii == nk - 1))),
                        reads=['ones', 'Pt%d' % pti], writes=[PS[pd_i]])
                    cur_s = nxt_s
                P.op('dve', lambda: nc.vector.reciprocal(out=rden[:, :Wt], in_=pd[:, :Wt]), reads=[PS[pd_i]], writes=['rden'])
                P.op('dve', (lambda hd=hd: nc.vector.tensor_tensor(out=y3[:, hd, :], in0=po[:, :Wt], in1=rden[:, :Wt], op=ALU.mult)),
                     reads=[PS[po_i], 'rden'], writes=['ay_%d' % hd])

            for hd in range(8):
                do_head(hd)
                if hd == 3 and qi + 1 < len(qtiles):
                    p2_norm(qi + 1)
            outproj(y3, 'ay', wo3, 'awo', x3, AX, l, j, Wt, [0, 1])
            store_x(x3, AX, c0, Wt, 'd_axo')

        tiles = seq_tiles(W)
        for (c0, Wt, j) in tiles:
            pass1(c0, Wt, j)
        qtiles = [t_ for t_ in tiles if not (t_[2] == 1 and not need_ctx)]
        p2_load(0)
        p2_norm(0)
        for qi in range(len(qtiles)):
            pass2(qi)
        P.barrier()
        A.reset(m0)

    outbufs = []
    for li, l in enumerate(layers):
        last = do_final and (li == len(layers) - 1)
        if do_mixer:
            kind = l % 3
            need_ctx = l < NL - 1
            if kind == 0:
                gla_layer(l, l // 3, cur['ap'], cur['name'], need_ctx)
            if kind == 1:
                lru_layer(l, cur['ap'], cur['name'])
            if kind == 2:
                attn_layer(l, cur['ap'], cur['name'], need_ctx)
            cur = {'ap': xs, 'name': 'xs'}
        if cfg.get('mlp', True):
            mlp_sweep(l, last, cur['ap'], cur['name'])
            cur = {'ap': xs, 'name': 'xs'}

    if not do_final:
        for i in range(17):
            P.op('sp', (lambda i=i: nc.sync.dma_start(out=out_d[:, i * 256:(i + 1) * 256],
                                                       in_=xs[:, i * 256:(i + 1) * 256])),
                 reads=['xs_%d' % i], writes=['out_%d' % i], lane='d_cp%d' % (i % 4))
            outbufs.append('out_%d' % i)
    P.op('sp', None, reads=outbufs)
    P.emit(nc)
    es.close()
    return nc


_CACHE = {}


def make_in_maps(inputs):
    x = np.asarray(inputs['x'], np.float32)
    ctx = np.asarray(inputs['ctx'], np.float32)
    c = np.asarray(inputs['c'], np.float32)
    c_ctx = np.asarray(inputs['c_ctx'], np.float32)
    vecs = pack_vecs(inputs)
    consts = make_consts()
    rope = make_rope()
    maps = []
    for b in range(x.shape[0]):
        xT = np.ascontiguousarray(np.concatenate([ctx[b], x[b]], axis=0).T)
        cc = np.stack([c[b], c_ctx], axis=1)
        c2 = np.ascontiguousarray(cc.reshape(8, 128, 2).transpose(1, 0, 2).reshape(128, 16))
        maps.append({
            'xT': xT, 'c2': c2, 'vecs': vecs,
            'w_mod': np.asarray(inputs['w_mod'], np.float32),
            'w_mlp1': np.asarray(inputs['w_mlp1'], np.float32),
            'w_mlp2': np.asarray(inputs['w_mlp2'], np.float32),
            'consts': consts, 'rope': rope,
            **{k: np.asarray(inputs[k], np.float32) for k in (
                'gla_w_in', 'gla_w_up_f', 'gla_w_up_b', 'gla_w_o', 'lru_w_in', 'lru_wa_f', 'lru_wx_f',
                'lru_wa_b', 'lru_wx_b', 'lru_w_o', 'attn_w_in', 'attn_w_o')},
        })
    return maps


def run(inputs, cfg, cores=8):
    key = repr(sorted(cfg.items()))
    if key not in _CACHE:
        _CACHE[key] = build(cfg)
    nc = _CACHE[key]
    maps = make_in_maps(inputs)[:cores]
    res = run_bass_kernel_spmd(nc, maps, core_ids=list(range(cores)))
    outs = [np.asarray(r['outT']).T for r in res.results]
    return np.stack(outs, axis=0)


def kernel(**inputs):
    return run(inputs, {}).astype(np.float32)
```
